# Optimizing a Trainium2 kernel written in Bass

```python
import math
import jax, jax.numpy as jnp
from jax import lax
import numpy as np

D_MODEL = 2048
BATCH = 4
SEQ = 4096
DEPTH = 1

HEAD_DIM = 128
N_HEADS_TOTAL = D_MODEL // HEAD_DIM
N_DIFF_HEADS = N_HEADS_TOTAL // 2
N_MOBA_HEADS = N_HEADS_TOTAL - N_DIFF_HEADS
DIFF_SUB = HEAD_DIM // 2
DIFF_W = N_DIFF_HEADS * HEAD_DIM
MOBA_W = N_MOBA_HEADS * HEAD_DIM
MIX_W = DIFF_W + MOBA_W
IN_COLS = 3 * DIFF_W + 3 * MOBA_W
ROPE_THETA = 500000.0
ROT_FRACTION = 4
MOBA_BLOCK = 256
MOBA_TOPK = 3
MOBA_Q_CHUNK = 32
DIFF_Q_CHUNK = 128
FFN_DIM = 5632
CONV_W = 3
LN_EPS = 1e-5
RMS_EPS = 1e-5
DEEPNORM_ALPHA = (2.0 * DEPTH) ** 0.25
DEEPNORM_BETA = (8.0 * DEPTH) ** -0.25
DIFF_SCALE = DIFF_SUB ** -0.5
MOBA_SCALE = HEAD_DIM ** -0.5

kernel_name = "hymba_diffattn_moba_convffn_deepnorm"


def rope_tables(seq, rot_dim):
    inv = 1.0 / (ROPE_THETA ** (jnp.arange(0, rot_dim, 2, dtype=jnp.float32) / rot_dim))
    pos = jnp.arange(seq, dtype=jnp.float32)
    ang = pos[:, None] * inv[None, :]
    return jnp.cos(ang), jnp.sin(ang)


def apply_partial_rope(x, cos, sin):
    half = cos.shape[-1]
    rot = 2 * half
    x1, x2, xp = x[..., :half], x[..., half:rot], x[..., rot:]
    c = cos.astype(x.dtype)
    s = sin.astype(x.dtype)
    return jnp.concatenate([x1 * c - x2 * s, x2 * c + x1 * s, xp], axis=-1)


def layer_norm(x, g, b):
    xf = x.astype(jnp.float32)
    mu = jnp.mean(xf, axis=-1, keepdims=True)
    var = jnp.mean(jnp.square(xf - mu), axis=-1, keepdims=True)
    y = (xf - mu) * lax.rsqrt(var + LN_EPS)
    return (y * g.astype(jnp.float32) + b.astype(jnp.float32)).astype(x.dtype)


def diff_attention(q, k, v, lam, lam_init, subln_g, cos, sin):
    B, S = q.shape[0], q.shape[1]
    H = N_DIFF_HEADS
    q = q.reshape(B, S, H, 2, DIFF_SUB).transpose(0, 2, 3, 1, 4)
    k = k.reshape(B, S, H, 2, DIFF_SUB).transpose(0, 2, 3, 1, 4)
    v = v.reshape(B, S, H, HEAD_DIM).transpose(0, 2, 1, 3)
    q = apply_partial_rope(q, cos, sin)
    k = apply_partial_rope(k, cos, sin)
    C = DIFF_Q_CHUNK
    nq = S // C
    q_blocks = jnp.moveaxis(q.reshape(B, H, 2, nq, C, DIFF_SUB), 3, 0)
    kpos = jnp.arange(S)

    def one_block(args):
        q_blk, i = args
        s = jnp.einsum('bhjcd,bhjkd->bhjck', q_blk, k,
                       preferred_element_type=jnp.float32) * DIFF_SCALE
        qpos = i * C + jnp.arange(C)
        s = jnp.where(kpos[None, :] <= qpos[:, None], s, -jnp.inf)
        p = jax.nn.softmax(s, axis=-1)
        a = p[:, :, 0] - lam * p[:, :, 1]
        return jnp.einsum('bhck,bhkd->bhcd', a.astype(v.dtype), v)

    o = lax.map(one_block, (q_blocks, jnp.arange(nq)))
    o = jnp.moveaxis(o, 0, 2).reshape(B, H, S, HEAD_DIM)
    of = o.astype(jnp.float32)
    of = of * lax.rsqrt(jnp.mean(jnp.square(of), axis=-1, keepdims=True) + RMS_EPS)
    of = of * subln_g.astype(jnp.float32) * (1.0 - lam_init)
    return of.astype(v.dtype).transpose(0, 2, 1, 3).reshape(B, S, DIFF_W)


def moba_attention(q, k, v, cos, sin):
    B, S = q.shape[0], q.shape[1]
    H = N_MOBA_HEADS
    q = q.reshape(B, S, H, HEAD_DIM).transpose(0, 2, 1, 3)
    k = k.reshape(B, S, H, HEAD_DIM).transpose(0, 2, 1, 3)
    v = v.reshape(B, S, H, HEAD_DIM).transpose(0, 2, 1, 3)
    q = apply_partial_rope(q, cos, sin)
    k = apply_partial_rope(k, cos, sin)
    NB = -(-S // MOBA_BLOCK)
    S_pad = NB * MOBA_BLOCK
    K_SEL = min(MOBA_TOPK, NB)
    pad = ((0, 0), (0, 0), (0, S_pad - S), (0, 0))
    k_pad = jnp.pad(k, pad)
    v_pad = jnp.pad(v, pad)
    kb = k_pad.reshape(B, H, NB, MOBA_BLOCK, HEAD_DIM)
    vb = v_pad.reshape(B, H, NB, MOBA_BLOCK, HEAD_DIM)
    counts = jnp.minimum(MOBA_BLOCK, S - jnp.arange(NB) * MOBA_BLOCK).astype(jnp.float32)
    kmean = jnp.sum(kb.astype(jnp.float32), axis=3) / counts[None, None, :, None]
    gate = jnp.einsum('bhsd,bhnd->bhsn', q.astype(jnp.float32), kmean)
    qblk = jnp.arange(S) // MOBA_BLOCK
    past = jnp.arange(NB)[None, :] < qblk[:, None]
    gate = jnp.where(past, gate, -jnp.inf)
    top_val, top_idx = lax.top_k(gate, K_SEL)
    top_valid = jnp.isfinite(top_val)

    C = MOBA_Q_CHUNK
    nc = S // C
    q_c = jnp.moveaxis(q.reshape(B, H, nc, C, HEAD_DIM), 2, 0)
    idx_c = jnp.moveaxis(top_idx.reshape(B, H, nc, C, K_SEL), 2, 0)
    val_c = jnp.moveaxis(top_valid.reshape(B, H, nc, C, K_SEL), 2, 0)
    bi = jnp.arange(B)[:, None, None, None]
    hi = jnp.arange(H)[None, :, None, None]

    def one_chunk(args):
        qc, ic, vc, i = args
        start = i * C
        blk_start = (start // MOBA_BLOCK) * MOBA_BLOCK
        k_own = lax.dynamic_slice_in_dim(k_pad, blk_start, MOBA_BLOCK, axis=2)
        v_own = lax.dynamic_slice_in_dim(v_pad, blk_start, MOBA_BLOCK, axis=2)
        qpos = start + jnp.arange(C)
        kpos = blk_start + jnp.arange(MOBA_BLOCK)
        s_own = jnp.einsum('bhcd,bhkd->bhck', qc, k_own,
                           preferred_element_type=jnp.float32) * MOBA_SCALE
        s_own = jnp.where(kpos[None, :] <= qpos[:, None], s_own, -jnp.inf)
        k_sel = kb[bi, hi, ic]
        v_sel = vb[bi, hi, ic]
        s_sel = jnp.einsum('bhcd,bhcjkd->bhcjk', qc, k_sel,
                           preferred_element_type=jnp.float32) * MOBA_SCALE
        s_sel = jnp.where(vc[..., None], s_sel, -jnp.inf)
        s_sel = s_sel.reshape(B, H, C, K_SEL * MOBA_BLOCK)
        p = jax.nn.softmax(jnp.concatenate([s_own, s_sel], axis=-1), axis=-1)
        p_own = p[..., :MOBA_BLOCK].astype(v.dtype)
        p_sel = p[..., MOBA_BLOCK:].reshape(B, H, C, K_SEL, MOBA_BLOCK).astype(v.dtype)
        return (jnp.einsum('bhck,bhkd->bhcd', p_own, v_own)
                + jnp.einsum('bhcjk,bhcjkd->bhcd', p_sel, v_sel))

    o = lax.map(one_chunk, (q_c, idx_c, val_c, jnp.arange(nc)))
    o = jnp.moveaxis(o, 0, 2).reshape(B, H, S, HEAD_DIM)
    return o.transpose(0, 2, 1, 3).reshape(B, S, MOBA_W)


def conv_ffn(h, w_up, conv_w, conv_b, w_down):
    u = h @ w_up
    g, val = u[..., :FFN_DIM], u[..., FFN_DIM:]
    S = h.shape[1]
    gp = jnp.pad(g, ((0, 0), (CONV_W - 1, 0), (0, 0)))
    gc = conv_b + sum(conv_w[j] * gp[:, j:j + S] for j in range(CONV_W))
    return (jax.nn.silu(gc) * val) @ w_down


def setup_inputs(seed: int = 0) -> dict:
    key = jax.random.key(seed)
    ks = jax.random.split(key, 16)
    f32 = jnp.float32
    x = jax.random.normal(ks[0], (BATCH, SEQ, D_MODEL), f32)
    w_in = jax.random.normal(ks[1], (DEPTH, D_MODEL, IN_COLS), f32) * D_MODEL ** -0.5
    col_scale = np.ones((IN_COLS,), np.float32)
    col_scale[2 * DIFF_W:3 * DIFF_W] = DEEPNORM_BETA
    col_scale[3 * DIFF_W + 2 * MOBA_W:] = DEEPNORM_BETA
    w_in = w_in * jnp.asarray(col_scale)
    lambda_q1 = jax.random.normal(ks[2], (DEPTH, DIFF_SUB), f32) * 0.1
    lambda_k1 = jax.random.normal(ks[3], (DEPTH, DIFF_SUB), f32) * 0.1
    lambda_q2 = jax.random.normal(ks[4], (DEPTH, DIFF_SUB), f32) * 0.1
    lambda_k2 = jax.random.normal(ks[5], (DEPTH, DIFF_SUB), f32) * 0.1
    subln_g = 1.0 + 0.02 * jax.random.normal(ks[6], (DEPTH, HEAD_DIM), f32)
    w_out = jax.random.normal(ks[7], (DEPTH, MIX_W, D_MODEL), f32) * (MIX_W ** -0.5 * DEEPNORM_BETA)
    ln1_g = 1.0 + 0.02 * jax.random.normal(ks[8], (DEPTH, D_MODEL), f32)
    ln1_b = 0.02 * jax.random.normal(ks[9], (DEPTH, D_MODEL), f32)
    w_up = jax.random.normal(ks[10], (DEPTH, D_MODEL, 2 * FFN_DIM), f32) * (D_MODEL ** -0.5 * DEEPNORM_BETA)
    conv_w = jax.random.normal(ks[11], (DEPTH, CONV_W, FFN_DIM), f32) * CONV_W ** -0.5
    conv_b = 0.02 * jax.random.normal(ks[12], (DEPTH, FFN_DIM), f32)
    w_down = jax.random.normal(ks[13], (DEPTH, FFN_DIM, D_MODEL), f32) * (FFN_DIM ** -0.5 * DEEPNORM_BETA)
    ln2_g = 1.0 + 0.02 * jax.random.normal(ks[14], (DEPTH, D_MODEL), f32)
    ln2_b = 0.02 * jax.random.normal(ks[15], (DEPTH, D_MODEL), f32)
    return {"x": x, "w_in": w_in, "lambda_q1": lambda_q1, "lambda_k1": lambda_k1,
            "lambda_q2": lambda_q2, "lambda_k2": lambda_k2, "subln_g": subln_g,
            "w_out": w_out, "ln1_g": ln1_g, "ln1_b": ln1_b, "w_up": w_up,
            "conv_w": conv_w, "conv_b": conv_b, "w_down": w_down,
            "ln2_g": ln2_g, "ln2_b": ln2_b}


def reference(x, w_in, lambda_q1, lambda_k1, lambda_q2, lambda_k2, subln_g, w_out,
              ln1_g, ln1_b, w_up, conv_w, conv_b, w_down, ln2_g, ln2_b):
    S = x.shape[1]
    cos_d, sin_d = rope_tables(S, DIFF_SUB // ROT_FRACTION)
    cos_m, sin_m = rope_tables(S, HEAD_DIM // ROT_FRACTION)
    h = x
    for l in range(DEPTH):
        lam_init = 0.8 - 0.6 * math.exp(-0.3 * l)
        lam = (jnp.exp(jnp.sum(lambda_q1[l].astype(jnp.float32) * lambda_k1[l].astype(jnp.float32)))
               - jnp.exp(jnp.sum(lambda_q2[l].astype(jnp.float32) * lambda_k2[l].astype(jnp.float32)))
               + lam_init)
        proj = h @ w_in[l]
        o0 = 0
        q_d = proj[..., o0:o0 + DIFF_W]; o0 += DIFF_W
        k_d = proj[..., o0:o0 + DIFF_W]; o0 += DIFF_W
        v_d = proj[..., o0:o0 + DIFF_W]; o0 += DIFF_W
        q_m = proj[..., o0:o0 + MOBA_W]; o0 += MOBA_W
        k_m = proj[..., o0:o0 + MOBA_W]; o0 += MOBA_W
        v_m = proj[..., o0:o0 + MOBA_W]
        a_out = diff_attention(q_d, k_d, v_d, lam, lam_init, subln_g[l], cos_d, sin_d)
        b_out = moba_attention(q_m, k_m, v_m, cos_m, sin_m)
        mix = jnp.concatenate([a_out, b_out], axis=-1) @ w_out[l]
        h = layer_norm(DEEPNORM_ALPHA * h + mix, ln1_g[l], ln1_b[l])
        f = conv_ffn(h, w_up[l], conv_w[l], conv_b[l], w_down[l])
        h = layer_norm(DEEPNORM_ALPHA * h + f, ln2_g[l], ln2_b[l])
    return h
```

```python
import os
import numpy as np
import ml_dtypes
import concourse.bass as bass
import concourse.mybir as mybir
from concourse.bass_utils import run_bass_kernel_spmd

F32 = mybir.dt.float32
BF16 = mybir.dt.bfloat16
U8 = mybir.dt.uint8
ALU = mybir.AluOpType
AF = mybir.ActivationFunctionType
AX = mybir.AxisListType

S = 4096
D = 2048
NT = 32
NQT = 17
QW = 2080
FF = 5632
NFC = 44
BIG = 30720.0
ALPHA = 2.0 ** 0.25
LAM_INIT = 0.2
DIFF_SCALE = 64 ** -0.5
MOBA_SCALE = 128 ** -0.5
LN_EPS = 1e-5
RMS_EPS = 1e-5

ENGS = ("pe", "act", "dve", "pool", "sp")
RING = 8


class Prog:
    def __init__(self, nc, esem, rings):
        self.nc = nc
        self.esem = esem
        self.rings = rings
        self.sigcount = {e: 0 for e in ENGS}
        self.dmacount = {q: 0 for q in rings}
        self.waited = {e: {} for e in ENGS}
        self.reset()

    def reset(self):
        self.ops = []
        self.lastw = {}
        self.readers = {}

    def _deps(self, reads, writes):
        writes = list(writes) + [t for t in reads if t.startswith("ps") and t not in writes]
        deps = {}
        for t in reads:
            w = self.lastw.get(t)
            if w is not None:
                deps[w] = True
        for t in writes:
            w = self.lastw.get(t)
            if w is not None:
                deps.setdefault(w, False)
            for r in self.readers.get(t, ()):
                deps.setdefault(r, False)
        idx = len(self.ops)
        for t in writes:
            self.lastw[t] = idx
            self.readers[t] = []
        for t in reads:
            if t not in writes:
                self.readers.setdefault(t, []).append(idx)
        return deps

    def add(self, eng, fn, reads=(), writes=()):
        deps = self._deps(reads, writes)
        self.ops.append(dict(eng=eng, fn=fn, deps=deps, dma=False, signal=False))

    def dma(self, q, out, in_, reads=(), writes=()):
        deps = self._deps(reads, writes)
        k = self.dmacount[q]
        self.dmacount[q] = k + 1
        self.ops.append(dict(eng=q, fn=None, out=out, in_=in_, deps=deps, dma=True, k=k,
                             sem=self.rings[q][k % RING], val=16 * (k // RING + 1)))

    def emit(self):
        ops = self.ops
        for o in ops:
            need = []
            for pi, raw in o["deps"].items():
                p = ops[pi]
                if p["dma"]:
                    need.append(pi)
                elif (not o["dma"]) and p["eng"] == o["eng"]:
                    if o["eng"] != "pe" and raw:
                        need.append(pi)
                else:
                    need.append(pi)
            o["need"] = need
            for pi in need:
                if not ops[pi]["dma"]:
                    ops[pi]["signal"] = True
        for o in ops:
            if (not o["dma"]) and o["signal"]:
                self.sigcount[o["eng"]] += 1
                o["sem"] = self.esem[o["eng"]]
                o["val"] = self.sigcount[o["eng"]]
        streams = {e: [o for o in ops if o["eng"] == e] for e in ENGS}
        nc = self.nc
        with nc.Block() as block:
            hmap = {"pe": block.tensor, "act": block.scalar, "dve": block.vector,
                    "pool": block.gpsimd, "sp": block.sync}
            for e in ENGS:
                st = streams[e]
                if not st and e not in self.rings:
                    continue

                def body(eh, e=e, st=st):
                    wd = self.waited[e]

                    def wait(sem, val):
                        key = id(sem)
                        if wd.get(key, 0) < val:
                            eh.wait_ge(sem, val)
                            wd[key] = val

                    for o in st:
                        for pi in o["need"]:
                            p = ops[pi]
                            wait(p["sem"], p["val"])
                        if o["dma"]:
                            if o["k"] >= RING:
                                wait(o["sem"], o["val"] - 16)
                            eh.dma_start(out=o["out"], in_=o["in_"]).then_inc(o["sem"], 16)
                        else:
                            ins = o["fn"](eh)
                            if o["signal"]:
                                ins.then_inc(o["sem"], 1)
                    if e in self.rings:
                        k = self.dmacount[e]
                        for slot in range(RING):
                            n = (k - slot + RING - 1) // RING if k > slot else 0
                            if n > 0:
                                wait(self.rings[e][slot], 16 * n)

                hmap[e](body)
        self.reset()


class Arena:
    def __init__(self, buf, size):
        self.buf = buf
        self.size = size
        self.off = 0

    def reset(self, off=0):
        self.off = off

    def alloc(self, shape, dt):
        esz = 2 if dt == BF16 else 4
        n = esz
        for s in shape[1:]:
            n *= s
        n = (n + 63) // 64 * 64
        assert self.off + n <= self.size, ("SBUF arena overflow", self.off, n, self.size)
        v = self.buf[:, self.off:self.off + n].bitcast(dt)
        self.off += n
        cnt = 1
        for s in shape[1:]:
            cnt *= s
        v = v[:, 0:cnt]
        if len(shape) == 3:
            v = v.rearrange("p (a b) -> p a b", a=shape[1])
        elif len(shape) == 4:
            v = v.rearrange("p (a b c) -> p a b c", a=shape[1], b=shape[2])
        return v


def bc(ap, shape):
    return ap.to_broadcast(list(shape))


def phase_A(C):
    nc, P, ar, ps = C["nc"], C["P"], C["arena"], C["ps"]
    dr = C["dram"]
    ar.reset()
    Wb = [ar.alloc([128, 16, 1024], BF16) for _ in range(2)]
    Xb = [ar.alloc([128, 16, 512], BF16) for _ in range(3)]
    tabk = ar.alloc([128, 32, 96], F32)
    tabq = ar.alloc([128, 17, 96], F32)
    KR = [ar.alloc([128, 1024], BF16) for _ in range(2)]
    T1 = [ar.alloc([128, 256], F32) for _ in range(2)]
    T2 = [ar.alloc([128, 256], F32) for _ in range(2)]
    KTs = [ar.alloc([128, 8, 512], BF16) for _ in range(2)]
    Vs = [ar.alloc([128, 4, 1024], BF16) for _ in range(2)]
    identb = ar.alloc([128, 128], BF16)

    P.dma("sp", tabk, dr["tabk"], writes=["tabk"])
    P.dma("sp", tabq, dr["tabq"], writes=["tabq"])
    P.dma("sp", identb, dr["identb"], writes=["identb"])

    w_in = dr["w_in"].rearrange("(kc p) n -> p kc n", p=128)
    xT = dr["xT"].rearrange("(kc p) t -> p kc t", p=128)
    xTq = dr["xTq"].rearrange("(kc p) t -> p kc t", p=128)
    Vscr = dr["Vscr"].rearrange("(kt p) c -> p kt c", p=128)

    passes = [
        ("K", "d", 1024, 0), ("V", None, 2048, 0), ("K", "m", 4096, 8), ("V", None, 5120, 8),
        ("Q", "d", 0, 0), ("Q", "m", 3072, 8),
    ]
    cnt = dict(w=0, x=0, tile=0, kr=0, st=0)
    _sel = os.environ.get("MK_A_PASSES")
    if _sel:
        passes = [passes[int(i)] for i in _sel.split(",")]
    _nb = int(os.environ.get("MK_A_BLOCKS", "99"))
    _norope = bool(os.environ.get("MK_A_NOROPE"))
    for (kind, rk, col0, h0) in passes:
        wslot = cnt["w"] % 2
        cnt["w"] += 1
        for half in range(2):
            P.dma("pool", Wb[wslot][:, :, half * 512:(half + 1) * 512],
                  w_in[:, :, col0 + half * 512: col0 + (half + 1) * 512],
                  writes=[f"Wb{wslot}"])
        if kind == "Q":
            blocks = [(i * 512, 512) for i in range(4)] + [(2048, 32)]
            xsrc, tab = xTq, tabq
        else:
            blocks = [(i * 512, 512) for i in range(8)]
            xsrc, tab = xT, tabk
        for (tok0, ntok) in blocks[:_nb]:
            xslot = cnt["x"] % 3
            cnt["x"] += 1
            P.dma("pool", Xb[xslot][:, :, 0:ntok], xsrc[:, :, tok0:tok0 + ntok], writes=[f"Xb{xslot}"])
            sslot = cnt["st"] % 2
            cnt["st"] += 1
            ntile = (ntok + 127) // 128
            for t in range(ntile):
                rows = min(128, ntok - t * 128)
                gt = (tok0 + t * 128) // 128
                s = cnt["tile"] % 2
                cnt["tile"] += 1
                b0 = 2 * s
                for cg in range(2):
                    for kc in range(16):
                        P.add("pe", lambda e, o=ps[0:rows, b0 + cg, :], l=Xb[xslot][:, kc, t * 128:t * 128 + rows],
                              r=Wb[wslot][:, kc, cg * 512:(cg + 1) * 512], st=(kc == 0), sp=(kc == 15):
                              e.matmul(o, l, r, start=st, stop=sp),
                              reads=[f"Xb{xslot}", f"Wb{wslot}"], writes=[f"ps{b0 + cg}"])
                if kind == "V":
                    for cg in range(2):
                        P.add("act", lambda e, o=Vs[sslot][0:rows, t, cg * 512:(cg + 1) * 512], i=ps[0:rows, b0 + cg, :]:
                              e.activation(o, i, AF.Copy),
                              reads=[f"ps{b0 + cg}"], writes=[f"Vs{sslot}"])
                else:
                    kr = cnt["kr"] % 2
                    cnt["kr"] += 1
                    for cg in range(2):
                        P.add("act", lambda e, o=KR[kr][0:rows, cg * 512:(cg + 1) * 512], i=ps[0:rows, b0 + cg, :]:
                              e.activation(o, i, AF.Copy),
                              reads=[f"ps{b0 + cg}"], writes=[f"KR{kr}"])
                    if rk == "m":
                        U, W_, R_ = 8, 128, 32
                        cofs, sofs = 0, 32
                    else:
                        U, W_, R_ = 16, 64, 16
                        cofs, sofs = 64, 80
                    H_ = R_ // 2
                    UB = U // 2
                    for bk in range(2):
                        psv = ps[0:rows, b0 + bk, :].rearrange("p (u d) -> p u d", d=W_)
                        t1 = T1[kr][0:rows, bk * UB * R_:(bk + 1) * UB * R_].rearrange("p (u d) -> p u d", d=R_)
                        t2 = T2[kr][0:rows, bk * UB * R_:(bk + 1) * UB * R_].rearrange("p (u d) -> p u d", d=R_)
                        krv = KR[kr][0:rows, bk * 512:(bk + 1) * 512].rearrange("p (u d) -> p u d", d=W_)
                        ctab = tab[0:rows, gt, cofs:cofs + R_].unsqueeze(1)
                        stab_a = tab[0:rows, gt, sofs:sofs + H_].unsqueeze(1)
                        stab_b = tab[0:rows, gt, sofs + H_:sofs + R_].unsqueeze(1)
                        rd = [f"ps{b0 + bk}", "tabk", "tabq"]
                        if _norope:
                            continue
                        P.add("dve", lambda e, o=t1, a=psv[:, :, 0:R_], b_=bc(ctab, [rows, UB, R_]):
                              e.tensor_tensor(o, a, b_, ALU.mult), reads=rd, writes=[f"T1{kr}{bk}"])
                        P.add("dve", lambda e, o=t2[:, :, 0:H_], a=psv[:, :, H_:R_], b_=bc(stab_a, [rows, UB, H_]):
                              e.tensor_tensor(o, a, b_, ALU.mult), reads=rd, writes=[f"T2a{kr}{bk}"])
                        P.add("dve", lambda e, o=t2[:, :, H_:R_], a=psv[:, :, 0:H_], b_=bc(stab_b, [rows, UB, H_]):
                              e.tensor_tensor(o, a, b_, ALU.mult), reads=rd, writes=[f"T2b{kr}{bk}"])
                        P.add("dve", lambda e, o=krv[:, :, 0:R_], a=t1, b_=t2:
                              e.tensor_tensor(o, a, b_, ALU.add),
                              reads=[f"T1{kr}{bk}", f"T2a{kr}{bk}", f"T2b{kr}{bk}"], writes=[f"KR{kr}"])
                    tb = 4 + (cnt["kr"] % 2)
                    psT = ps[:, tb, :].bitcast(BF16).rearrange("p (c t) -> p c t", c=8)
                    for c in range(8):
                        P.add("pe", lambda e, o=psT[:, c, 0:rows], i=KR[kr][0:rows, c * 128:(c + 1) * 128],
                              idn=identb[0:rows, 0:rows]: e.transpose(o, i, idn),
                              reads=[f"KR{kr}", "identb"], writes=[f"ps{tb}"])
                    P.add("dve", lambda e, o=KTs[sslot][:, :, t * 128:t * 128 + rows], i=psT[:, :, 0:rows]:
                          e.tensor_copy(o, i), reads=[f"ps{tb}"], writes=[f"KTs{sslot}"])
            if kind == "V":
                t0 = tok0 // 128
                P.dma("sp", Vscr[:, t0:t0 + 4, h0 * 128:h0 * 128 + 1024], Vs[sslot][:, :, :],
                      reads=[f"Vs{sslot}"], writes=["Vscr"])
            elif kind == "K":
                P.dma("sp", dr["KTscr"][h0:h0 + 8, :, tok0:tok0 + ntok].rearrange("h d t -> d h t"),
                      KTs[sslot][:, :, 0:ntok], reads=[f"KTs{sslot}"], writes=["KTscr"])
            else:
                P.dma("sp", dr["QTscr"][h0:h0 + 8, :, tok0:tok0 + ntok].rearrange("h d t -> d h t"),
                      KTs[sslot][:, :, 0:ntok], reads=[f"KTs{sslot}"], writes=["QTscr"])
    P.emit()


def phase_B(C):
    nc, P, ar, ps, dr = C["nc"], C["P"], C["arena"], C["ps"], C["dram"]
    ar.reset()
    KTh = [ar.alloc([128, S], BF16) for _ in range(2)]
    Vh = [ar.alloc([128, 32, 129], BF16) for _ in range(2)]
    QTh = [ar.alloc([128, QW], BF16) for _ in range(2)]
    NPT = 6
    pT = [ar.alloc([128, 4, 128], BF16) for _ in range(NPT)]
    negT = [ar.alloc([128, QW], BF16) for _ in range(2)]
    cmask = ar.alloc([128, 2, 128], BF16)
    hmask = ar.alloc([128, 32, 32], BF16)
    identb = ar.alloc([128, 128], BF16)
    esel = ar.alloc([128, 2048], BF16)
    pastb = ar.alloc([128, 17, 16], F32)
    ownb = ar.alloc([128, 17, 16], F32)
    kmf = ar.alloc([128, 16], F32)
    kmT = [ar.alloc([128, 16], BF16) for _ in range(2)]
    gm = ar.alloc([128, 17, 16], F32)
    top8 = ar.alloc([128, 17, 8], F32)
    thr = ar.alloc([128, 17], F32)
    sel = ar.alloc([128, 17, 16], F32)
    negm = ar.alloc([128, 17, 16], BF16)
    lamp = ar.alloc([128, 4, 64], F32)
    lprod = ar.alloc([128, 2, 64], F32)
    lsum = ar.alloc([128, 2], F32)
    lexp = ar.alloc([128, 2], F32)
    ltmp = ar.alloc([128, 1], F32)
    lamneg = ar.alloc([128, 1], F32)
    gsub = ar.alloc([128, 128], F32)
    mhalf = ar.alloc([128, 1], F32)
    rec = [ar.alloc([128, 2], F32) for _ in range(2)]
    rec1n = [ar.alloc([128, 1], F32) for _ in range(2)]
    Osb = [ar.alloc([128, 128], F32) for _ in range(2)]
    Ao = [ar.alloc([128, 128], F32) for _ in range(2)]
    junk = ar.alloc([128, 128], F32)
    ss = [ar.alloc([128, 1], F32) for _ in range(2)]
    ms = [ar.alloc([128, 1], F32) for _ in range(2)]
    rstd = [ar.alloc([128, 1], F32) for _ in range(2)]
    Ab = [ar.alloc([128, 128], BF16) for _ in range(2)]
    ATs = [ar.alloc([128, QW], BF16) for _ in range(2)]

    P.dma("sp", cmask, dr["cmask"], writes=["cmask"])
    P.dma("sp", hmask, dr["hmask"], writes=["hmask"])
    P.dma("sp", identb, dr["identb"], writes=["identb"])
    P.dma("sp", esel[0:16, :], dr["esel"], writes=["esel"])
    P.dma("sp", pastb, dr["pastb"], writes=["pastb"])
    P.dma("sp", ownb, dr["ownb"], writes=["ownb"])
    P.dma("sp", lamp, dr["lamp"].partition_broadcast(128), writes=["lamp"])
    P.dma("sp", gsub, dr["subg"].partition_broadcast(128), writes=["gsub"])
    for i in range(2):
        P.add("dve", lambda e, o=Vh[i][:, :, 128:129]: e.memset(o, 1.0), writes=[f"Vh{i}"])
    P.add("dve", lambda e: e.memset(mhalf, -0.5), writes=["mhalf"])
    lampv = lamp.rearrange("p (a b) d -> p a b d", b=2)
    P.add("dve", lambda e: e.tensor_tensor(lprod, lampv[:, :, 0, :], lampv[:, :, 1, :], ALU.mult),
          reads=["lamp"], writes=["lprod"])
    P.add("dve", lambda e: e.tensor_reduce(lsum, lprod, AX.X, ALU.add), reads=["lprod"], writes=["lsum"])
    P.add("act", lambda e: e.activation(lexp, lsum, AF.Exp), reads=["lsum"], writes=["lexp"])
    P.add("dve", lambda e: e.tensor_tensor(ltmp, lexp[:, 0:1], lexp[:, 1:2], ALU.subtract),
          reads=["lexp"], writes=["ltmp"])
    P.add("dve", lambda e: e.tensor_scalar(lamneg, ltmp, LAM_INIT, -1.0, ALU.add, ALU.mult),
          reads=["ltmp"], writes=["lamneg"])
    P.add("dve", lambda e: e.tensor_scalar(gsub, gsub, 1.0 - LAM_INIT, None, ALU.mult),
          reads=["gsub"], writes=["gsub"])

    Vscr = dr["Vscr"].rearrange("(kt p) c -> p kt c", p=128)
    cnt = dict(batch=0, pt=0, ep=0, tile=0)
    heads = C.get("heads", list(range(16)))
    for h in heads:
        hb = h % 2
        moba = h >= 8
        P.dma("sp", KTh[hb], dr["KTscr"][h], writes=[f"KTh{hb}"])
        P.dma("sp", QTh[hb], dr["QTscr"][h], writes=[f"QTh{hb}"])
        for hf in range(2):
            P.dma("sp", Vh[hb][:, hf * 16:(hf + 1) * 16, 0:128], Vscr[:, hf * 16:(hf + 1) * 16, h * 128:(h + 1) * 128],
                  writes=[f"Vh{hb}"])

        def qinfo(t):
            if t < 16:
                return 128, t * 128, 2 * t + 2
            return 32, 2048, 32

        if moba:
            P.add("dve", lambda e, i=KTh[hb].rearrange("p (n k) -> p n k", k=256): e.tensor_reduce(kmf, i, AX.X, ALU.add),
                  reads=[f"KTh{hb}"], writes=["kmf"])
            P.add("dve", lambda e, o=kmT[hb]: e.tensor_scalar(o, kmf, 1.0 / 256.0, None, ALU.mult),
                  reads=["kmf"], writes=[f"kmT{hb}"])
            psG = ps[:, 6, 0:272].rearrange("p (t n) -> p t n", n=16)
            for t in range(17):
                nq, c0, _ = qinfo(t)
                P.add("pe", lambda e, o=psG[0:nq, t, :], l=QTh[hb][:, c0:c0 + nq], r=kmT[hb], st=(t == 0), sp=(t == 16):
                      e.matmul(o, l, r, start=st, stop=sp, skip_group_check=True),
                      reads=[f"QTh{hb}", f"kmT{hb}"], writes=["ps6"])
            P.add("dve", lambda e: e.tensor_tensor(gm[:, 0:16, :], psG[:, 0:16, :], pastb[:, 0:16, :], ALU.add),
                  reads=["ps6", "pastb"], writes=["gm"])
            P.add("dve", lambda e: e.tensor_tensor(gm[0:32, 16, :], psG[0:32, 16, :], pastb[0:32, 16, :], ALU.add),
                  reads=["ps6", "pastb"], writes=["gmh"])
            for t in range(17):
                nq = 128 if t < 16 else 32
                P.add("dve", lambda e, o=top8[0:nq, t, :], i=gm[0:nq, t, :]: e.max(o, i),
                      reads=["gm", "gmh"], writes=[f"top8_{t}"])
            t8 = [f"top8_{t}" for t in range(17)]
            P.add("dve", lambda e: e.tensor_scalar(thr[:, 0:16], top8[:, 0:16, 2], -1e29, None, ALU.max),
                  reads=t8, writes=["thr"])
            P.add("dve", lambda e: e.tensor_scalar(thr[0:32, 16:17], top8[0:32, 16, 2:3], -1e29, None, ALU.max),
                  reads=t8, writes=["thrh"])
            P.add("dve", lambda e: e.tensor_tensor(sel[:, 0:16, :], gm[:, 0:16, :],
                                                   bc(thr[:, 0:16].unsqueeze(2), [128, 16, 16]), ALU.is_ge),
                  reads=["gm", "thr"], writes=["sel"])
            P.add("dve", lambda e: e.tensor_tensor(sel[0:32, 16, :], gm[0:32, 16, :],
                                                   bc(thr[0:32, 16:17], [32, 16]), ALU.is_ge),
                  reads=["gmh", "thrh"], writes=["selh"])
            P.add("dve", lambda e: e.scalar_tensor_tensor(negm[:, 0:16, :], sel[:, 0:16, :], BIG, ownb[:, 0:16, :],
                                                          ALU.mult, ALU.add),
                  reads=["sel", "ownb"], writes=["negm"])
            P.add("dve", lambda e: e.scalar_tensor_tensor(negm[0:32, 16, :], sel[0:32, 16, :], BIG, ownb[0:32, 16, :],
                                                          ALU.mult, ALU.add),
                  reads=["selh", "ownb"], writes=["negmh"])
            psNT = ps[:, 6, :].bitcast(BF16)
            for g, tiles in enumerate([list(range(0, 8)), list(range(8, 16)), [16]]):
                for i, t in enumerate(tiles):
                    nq = 128 if t < 16 else 32
                    P.add("pe", lambda e, o=psNT[0:16, i * 128:i * 128 + nq], i_=negm[0:nq, t, :], idn=identb[0:nq, 0:nq]:
                          e.transpose(o, i_, idn), reads=["negm", "negmh", "identb"], writes=["ps6"])
                ncols = 1024 if g < 2 else 32
                P.add("dve", lambda e, o=negT[hb][0:16, g * 1024:g * 1024 + ncols], i_=psNT[0:16, 0:ncols]:
                      e.tensor_copy(o, i_), reads=["ps6"], writes=[f"negT{hb}"])

        subs = [0] if moba else [0, 1]
        scale = MOBA_SCALE if moba else DIFF_SCALE
        pending = []
        deferred = []

        def flush_pending():
            for f in pending:
                f()
            pending.clear()

        def flush_deferred():
            for f in deferred:
                f()
            deferred.clear()

        psAT = ps[:, 7, :].bitcast(BF16).rearrange("p (c t) -> p c t", c=8)

        def epilogue(t, nq, c0, ob):
            eb = cnt["ep"] % 2
            cnt["ep"] += 1
            psO = ps[:, ob, 0:258].rearrange("p (s d) -> p s d", d=129)
            pt = f"ps{ob}"
            if moba:
                P.add("dve", lambda e: e.reciprocal(rec[eb][0:nq, 0:1], psO[0:nq, 0, 128:129]),
                      reads=[pt], writes=[f"rec{eb}"])
                P.add("dve", lambda e: e.tensor_scalar(Ab[eb][0:nq, :], psO[0:nq, 0, 0:128], rec[eb][0:nq, 0:1], None, ALU.mult),
                      reads=[pt, f"rec{eb}"], writes=[f"Ab{eb}"])
            else:
                P.add("dve", lambda e: e.reciprocal(rec[eb][0:nq, 0:2], psO[0:nq, :, 128]),
                      reads=[pt], writes=[f"rec{eb}"])
                P.add("dve", lambda e: e.tensor_scalar(rec1n[eb][0:nq, :], rec[eb][0:nq, 1:2], lamneg[0:nq, 0:1], None, ALU.mult),
                      reads=[f"rec{eb}", "lamneg"], writes=[f"rec1n{eb}"])
                P.add("dve", lambda e: e.tensor_scalar(Osb[eb][0:nq, :], psO[0:nq, 0, 0:128], rec[eb][0:nq, 0:1], None, ALU.mult),
                      reads=[pt, f"rec{eb}"], writes=[f"Osb{eb}"])
                P.add("dve", lambda e: e.scalar_tensor_tensor(Ao[eb][0:nq, :], psO[0:nq, 1, 0:128], rec1n[eb][0:nq, 0:1],
                                                              Osb[eb][0:nq, :], ALU.mult, ALU.add),
                      reads=[pt, f"rec1n{eb}", f"Osb{eb}"], writes=[f"Ao{eb}"])
                P.add("dve", lambda e: e.scalar_tensor_tensor(junk[0:nq, :], Ao[eb][0:nq, :], 1.0, Ao[eb][0:nq, :],
                                                              ALU.mult, ALU.mult, accum_out=ss[eb][0:nq, :]),
                      reads=[f"Ao{eb}"], writes=["junk", f"ss{eb}"])
                P.add("dve", lambda e: e.tensor_scalar(ms[eb][0:nq, :], ss[eb][0:nq, :], 1.0 / 128.0, RMS_EPS, ALU.mult, ALU.add),
                      reads=[f"ss{eb}"], writes=[f"ms{eb}"])
                P.add("pool", lambda e: e.tensor_tensor(rstd[eb][0:nq, :], ms[eb][0:nq, :], mhalf[0:nq, :], ALU.pow),
                      reads=[f"ms{eb}", "mhalf"], writes=[f"rstd{eb}"])
                P.add("dve", lambda e: e.scalar_tensor_tensor(Ab[eb][0:nq, :], Ao[eb][0:nq, :], rstd[eb][0:nq, 0:1],
                                                              gsub[0:nq, :], ALU.mult, ALU.mult),
                      reads=[f"Ao{eb}", f"rstd{eb}", "gsub"], writes=[f"Ab{eb}"])

            def tr():
                P.add("pe", lambda e, o=psAT[:, t % 8, 0:nq], i_=Ab[eb][0:nq, :], idn=identb[0:nq, 0:nq]:
                      e.transpose(o, i_, idn), reads=[f"Ab{eb}", "identb"], writes=["ps7"])
                if t in (7, 15, 16):
                    g0 = (t // 8) * 1024
                    ncols = 1024 if t < 16 else 32
                    P.add("act", lambda e, o=ATs[hb][:, g0:g0 + ncols],
                          i_=ps[:, 7, :].bitcast(BF16)[:, 0:ncols]: e.activation(o, i_, AF.Copy),
                          reads=["ps7"], writes=[f"ATs{hb}"])
            deferred.append(tr)

        for t in range(17):
            nq, c0, nkt = qinfo(t)
            ob = 4 + (cnt["tile"] % 2)
            cnt["tile"] += 1
            psO = ps[:, ob, 0:258].rearrange("p (s d) -> p s d", d=129)
            batches = [list(range(i, min(i + 4, nkt))) for i in range(0, nkt, 4)]
            for bi, batch in enumerate(batches):
                sset = cnt["batch"] % 2
                cnt["batch"] += 1
                nb = len(batch)
                psS = [ps[:, 2 * sset + s_, :].rearrange("p (i q) -> p i q", q=128) for s_ in range(2)]
                for i, kt in enumerate(batch):
                    for s_ in subs:
                        rs0 = 0 if moba else 64 * s_
                        rs1 = 128 if moba else 64 * s_ + 64
                        extra = moba or (t < 16 and kt >= 2 * t) or t == 16
                        P.add("pe", lambda e, o=psS[s_][:, i, 0:nq], l=KTh[hb][rs0:rs1, kt * 128:(kt + 1) * 128],
                              r=QTh[hb][rs0:rs1, c0:c0 + nq], st=(i == 0), sp=(not extra):
                              e.matmul(o, l, r, start=st, stop=sp, skip_group_check=True),
                              reads=[f"KTh{hb}", f"QTh{hb}"], writes=[f"ps{2 * sset + s_}"])
                    causal = (t < 16 and kt >= 2 * t) or t == 16
                    if moba:
                        blk = kt // 2
                        P.add("pe", lambda e, o=psS[0][:, i, 0:nq], l=esel[0:16, blk * 128:(blk + 1) * 128],
                              r=negT[hb][0:16, c0:c0 + nq], sp=(not causal):
                              e.matmul(o, l, r, start=False, stop=sp, skip_group_check=True),
                              reads=["esel", f"negT{hb}"], writes=[f"ps{2 * sset}"])
                    if causal:
                        mk = cmask[:, kt - 2 * t, :] if t < 16 else hmask[:, kt, :]
                        for s_ in subs:
                            P.add("pe", lambda e, o=psS[s_][:, i, 0:nq], r=mk:
                                  e.matmul(o, identb, r, start=False, stop=True, skip_group_check=True),
                                  reads=["identb", "cmask", "hmask"], writes=[f"ps{2 * sset + s_}"])
                slots = {}
                for s_ in subs:
                    pslot = cnt["pt"] % NPT
                    cnt["pt"] += 1
                    slots[s_] = pslot
                    P.add("act", lambda e, o=pT[pslot][:, 0:nb, 0:nq], i_=psS[s_][:, 0:nb, 0:nq], sc=scale:
                          e.activation(o, i_, AF.Exp, scale=sc),
                          reads=[f"ps{2 * sset + s_}"], writes=[f"pT{pslot}"])
                flush_pending()
                flush_deferred()

                def pv(batch=batch, slots=dict(slots), nq=nq, nkt=nkt, psO=psO, ob=ob, t=t, c0=c0, last=(bi == len(batches) - 1)):
                    for i, kt in enumerate(batch):
                        for s_ in subs:
                            P.add("pe", lambda e, o=psO[0:nq, s_, :], l=pT[slots[s_]][:, i, 0:nq], r=Vh[hb][:, kt, :],
                                  st=(kt == 0 and s_ == 0), sp=(kt == nkt - 1):
                                  e.matmul(o, l, r, start=st, stop=sp, skip_group_check=True),
                                  reads=[f"pT{slots[s_]}", f"Vh{hb}"], writes=[f"ps{ob}"])
                    if last:
                        epilogue(t, nq, c0, ob)
                pending.append(pv)
        flush_pending()
        flush_deferred()
        P.dma("sp", dr["ATscr"][h * 128:(h + 1) * 128, :], ATs[hb], reads=[f"ATs{hb}"], writes=["ATscr"])
    P.emit()


def _layernorm(P, y, rows, slot, T, lng, lnb, ytok):
    stats, mv, veps, rstd, nmr, mhalf = T["stats"][slot], T["mv"][slot], T["veps"][slot], T["rstd"][slot], T["nmr"][slot], T["mhalf"]
    for c in range(4):
        P.add("dve", lambda e, o=stats[0:rows, c, :], i_=y[0:rows, c * 512:(c + 1) * 512]: e.bn_stats(o, i_),
              reads=[ytok], writes=[f"stats{slot}_{c}"])
    P.add("dve", lambda e: e.bn_aggr(mv[0:rows, :], stats[0:rows, :, :].rearrange("p a b -> p (a b)")),
          reads=[f"stats{slot}_{c}" for c in range(4)], writes=[f"mv{slot}"])
    P.add("dve", lambda e: e.tensor_scalar(veps[0:rows, :], mv[0:rows, 1:2], LN_EPS, None, ALU.add),
          reads=[f"mv{slot}"], writes=[f"veps{slot}"])
    P.add("pool", lambda e: e.tensor_tensor(rstd[0:rows, :], veps[0:rows, :], mhalf[0:rows, :], ALU.pow),
          reads=[f"veps{slot}", "mhalf"], writes=[f"rstd{slot}"])
    P.add("dve", lambda e: e.tensor_scalar(nmr[0:rows, :], mv[0:rows, 0:1], rstd[0:rows, 0:1], -1.0, ALU.mult, ALU.mult),
          reads=[f"mv{slot}", f"rstd{slot}"], writes=[f"nmr{slot}"])
    P.add("act", lambda e: e.activation(y[0:rows, :], y[0:rows, :], AF.Identity, bias=nmr[0:rows, 0:1], scale=rstd[0:rows, 0:1]),
          reads=[ytok, f"nmr{slot}", f"rstd{slot}"], writes=[ytok])
    P.add("dve", lambda e: e.tensor_tensor(y[0:rows, :], y[0:rows, :], lng[0:rows, :], ALU.mult),
          reads=[ytok, "lng"], writes=[ytok])
    P.add("pool", lambda e: e.tensor_tensor(y[0:rows, :], y[0:rows, :], lnb[0:rows, :], ALU.add),
          reads=[ytok, "lnb"], writes=[ytok])


def _ln_temps(ar, P):
    T = dict(stats=[ar.alloc([128, 4, 6], F32) for _ in range(2)], mv=[ar.alloc([128, 2], F32) for _ in range(2)],
             veps=[ar.alloc([128, 1], F32) for _ in range(2)], rstd=[ar.alloc([128, 1], F32) for _ in range(2)],
             nmr=[ar.alloc([128, 1], F32) for _ in range(2)], mhalf=ar.alloc([128, 1], F32))
    P.add("dve", lambda e: e.memset(T["mhalf"], -0.5), writes=["mhalf"])
    return T


def phase_C(C):
    nc, P, ar, ps, dr = C["nc"], C["P"], C["arena"], C["ps"], C["dram"]
    ar.reset()
    Wo = ar.alloc([128, 16, D], BF16)
    ATb = ar.alloc([128, 16, QW], BF16)
    lng = ar.alloc([128, D], F32)
    lnb = ar.alloc([128, D], F32)
    xs = [ar.alloc([128, D], F32) for _ in range(2)]
    y = [ar.alloc([128, D], F32) for _ in range(2)]
    hb16 = [ar.alloc([128, D], BF16) for _ in range(2)]
    hTs = [ar.alloc([128, 16, 128], BF16) for _ in range(2)]
    identb = ar.alloc([128, 128], BF16)
    T = _ln_temps(ar, P)
    P.dma("sp", identb, dr["identb"], writes=["identb"])
    P.dma("sp", lng, dr["lnp"][0].partition_broadcast(128), writes=["lng"])
    P.dma("sp", lnb, dr["lnp"][1].partition_broadcast(128), writes=["lnb"])
    w_out = dr["w_out"].rearrange("(kc p) n -> p kc n", p=128)
    for cg in range(4):
        P.dma("pool", Wo[:, :, cg * 512:(cg + 1) * 512], w_out[:, :, cg * 512:(cg + 1) * 512], writes=["Wo"])
    ATv = dr["ATscr"].rearrange("(h p) t -> p h t", p=128)
    for q4 in range(4):
        P.dma("sp", ATb[:, q4 * 4:(q4 + 1) * 4, :], ATv[:, q4 * 4:(q4 + 1) * 4, :], writes=["ATb"])
    H1T = dr["H1Tscr"].rearrange("(kc p) t -> p kc t", p=128)
    for t in range(17):
        rows = 128 if t < 16 else 32
        c0 = t * 128
        sl = t % 2
        P.dma("sp", xs[sl][0:rows, :], dr["xq"][c0:c0 + rows, :], writes=[f"xs{sl}"])
        for cg in range(4):
            bank = cg
            for h in range(16):
                P.add("pe", lambda e, o=ps[0:rows, bank, :], l=ATb[:, h, c0:c0 + rows], r=Wo[:, h, cg * 512:(cg + 1) * 512],
                      st=(h == 0), sp=(h == 15): e.matmul(o, l, r, start=st, stop=sp),
                      reads=["ATb", "Wo"], writes=[f"ps{bank}"])
            P.add("dve", lambda e, o=y[sl][0:rows, cg * 512:(cg + 1) * 512], a=xs[sl][0:rows, cg * 512:(cg + 1) * 512],
                  b_=ps[0:rows, bank, :]: e.scalar_tensor_tensor(o, a, ALPHA, b_, ALU.mult, ALU.add),
                  reads=[f"xs{sl}", f"ps{bank}"], writes=[f"y{sl}"])
        _layernorm(P, y[sl], rows, sl, T, lng, lnb, f"y{sl}")
        P.dma("sp", dr["H1scr"][c0:c0 + rows, :], y[sl][0:rows, :], reads=[f"y{sl}"], writes=["H1scr"])
        P.add("act", lambda e, o=hb16[sl][0:rows, :], i_=y[sl][0:rows, :]: e.activation(o, i_, AF.Copy),
              reads=[f"y{sl}"], writes=[f"hb16{sl}"])
        for half in range(2):
            tb = 4 + 2 * sl + half
            psT = ps[:, tb, :].bitcast(BF16).rearrange("p (c t) -> p c t", c=8)
            for c in range(8):
                cc = half * 8 + c
                P.add("pe", lambda e, o=psT[:, c, 0:rows], i_=hb16[sl][0:rows, cc * 128:(cc + 1) * 128],
                      idn=identb[0:rows, 0:rows]: e.transpose(o, i_, idn),
                      reads=[f"hb16{sl}", "identb"], writes=[f"ps{tb}"])
            P.add("dve", lambda e, o=hTs[sl][:, half * 8:(half + 1) * 8, 0:rows], i_=psT[:, :, 0:rows]: e.tensor_copy(o, i_),
                  reads=[f"ps{tb}"], writes=[f"hTs{sl}"])
        P.dma("sp", H1T[:, :, c0:c0 + rows], hTs[sl][:, :, 0:rows], reads=[f"hTs{sl}"], writes=["H1Tscr"])
    P.emit()


def phase_D(C):
    nc, P, ar, ps, dr = C["nc"], C["P"], C["arena"], C["ps"], C["dram"]
    w_up = dr["w_up"].rearrange("(kc p) n -> p kc n", p=128)
    w_down = dr["w_down"].rearrange("(fc p) n -> p fc n", p=128)
    H1T = dr["H1Tscr"].rearrange("(kc p) t -> p kc t", p=128)
    for B in range(2):
        ar.reset()
        gT = ar.alloc([128, NFC, 1024], BF16)
        woff = ar.off
        Wu = [ar.alloc([128, 16, 1024], BF16) for _ in range(2)]
        hoff = ar.off
        hT = ar.alloc([128, 16, 1040], BF16)
        gsb = [ar.alloc([128, 4, 130], F32) for _ in range(2)]
        ccb = [ar.alloc([128, 4, 128], F32) for _ in range(2)]
        sgb = [ar.alloc([128, 4, 128], F32) for _ in range(2)]
        cwb = ar.alloc([128, NFC, 4], F32)
        hval = ar.alloc([128, 32], F32)
        P.dma("sp", cwb, dr["cwb"], writes=["cwb"])
        P.dma("sp", hval, dr["hvalid"], writes=["hval"])
        for q4 in range(4):
            P.dma("sp", hT[:, q4 * 4:(q4 + 1) * 4, 0:1024], H1T[:, q4 * 4:(q4 + 1) * 4, B * 1024:(B + 1) * 1024], writes=["hT"])
        P.dma("sp", hT[:, :, 1024:1040], H1T[:, :, 2048 + B * 16:2048 + (B + 1) * 16], writes=["hT"])
        ucnt = 0
        for fg in range(11):
            ws = fg % 2
            P.dma("pool", Wu[ws][:, :, 0:512], w_up[:, :, fg * 512:(fg + 1) * 512], writes=[f"W{ws}"])
            P.dma("pool", Wu[ws][:, :, 512:1024], w_up[:, :, FF + fg * 512:FF + (fg + 1) * 512], writes=[f"W{ws}"])
            for fcl in range(4):
                fc = fg * 4 + fcl
                for th in range(2):
                    st_ = ucnt % 2
                    ucnt += 1
                    bg, bv, bh = 2 * st_, 2 * st_ + 1, 4 + st_
                    for kc in range(16):
                        P.add("pe", lambda e, o=ps[:, bg, :], l=Wu[ws][:, kc, fcl * 128:(fcl + 1) * 128],
                              r=hT[:, kc, th * 512:(th + 1) * 512], st=(kc == 0), sp=(kc == 15):
                              e.matmul(o, l, r, start=st, stop=sp), reads=[f"W{ws}", "hT"], writes=[f"ps{bg}"])
                    for kc in range(16):
                        P.add("pe", lambda e, o=ps[:, bh, 0:8], l=Wu[ws][:, kc, fcl * 128:(fcl + 1) * 128],
                              r=hT[:, kc, 1024 + th * 8:1024 + (th + 1) * 8], st=(kc == 0), sp=(kc == 15):
                              e.matmul(o, l, r, start=st, stop=sp), reads=[f"W{ws}", "hT"], writes=[f"ps{bh}"])
                    for kc in range(16):
                        P.add("pe", lambda e, o=ps[:, bv, :], l=Wu[ws][:, kc, 512 + fcl * 128:512 + (fcl + 1) * 128],
                              r=hT[:, kc, th * 512:(th + 1) * 512], st=(kc == 0), sp=(kc == 15):
                              e.matmul(o, l, r, start=st, stop=sp), reads=[f"W{ws}", "hT"], writes=[f"ps{bv}"])
                    g = gsb[st_]
                    cc = ccb[st_]
                    sg = sgb[st_]
                    P.add("act", lambda e, o=g[:, :, 2:130], i_=ps[:, bg, :].rearrange("p (a b) -> p a b", b=128):
                          e.activation(o, i_, AF.Copy), reads=[f"ps{bg}"], writes=[f"gsbm{st_}"])
                    hv = hval[:, B * 16 + th * 8:B * 16 + (th + 1) * 8].rearrange("p (a b) -> p a b", b=2)
                    P.add("dve", lambda e, o=g[:, :, 0:2], a=ps[:, bh, 0:8].rearrange("p (a b) -> p a b", b=2), b_=hv:
                          e.tensor_tensor(o, a, b_, ALU.mult), reads=[f"ps{bh}", "hval"], writes=[f"gsbh{st_}"])
                    gr = [f"gsbm{st_}", f"gsbh{st_}", "cwb"]
                    P.add("dve", lambda e, o=cc, a=g[:, :, 2:130], s1=cwb[:, fc, 2:3], s2=cwb[:, fc, 3:4]:
                          e.tensor_scalar(o, a, s1, s2, ALU.mult, ALU.add), reads=gr, writes=[f"cc{st_}"])
                    P.add("dve", lambda e, o=cc, a=g[:, :, 1:129], s1=cwb[:, fc, 1:2]:
                          e.scalar_tensor_tensor(o, a, s1, o, ALU.mult, ALU.add), reads=gr + [f"cc{st_}"], writes=[f"cc{st_}"])
                    P.add("dve", lambda e, o=cc, a=g[:, :, 0:128], s1=cwb[:, fc, 0:1]:
                          e.scalar_tensor_tensor(o, a, s1, o, ALU.mult, ALU.add), reads=gr + [f"cc{st_}"], writes=[f"cc{st_}"])
                    P.add("act", lambda e, o=sg, i_=cc: e.activation(o, i_, AF.Silu), reads=[f"cc{st_}"], writes=[f"sg{st_}"])
                    P.add("dve", lambda e, o=gT[:, fc, th * 512:(th + 1) * 512].rearrange("p (a b) -> p a b", b=128), a=sg,
                          b_=ps[:, bv, :].rearrange("p (a b) -> p a b", b=128): e.tensor_tensor(o, a, b_, ALU.mult),
                          reads=[f"sg{st_}", f"ps{bv}"], writes=["gT"])
        P.emit()
        ar.reset(woff)
        Wd = []
        for i in range(2):
            Wd.append(ar.alloc([128, NFC, 256], BF16))
            ar.reset(woff + (i + 1) * 32768)
        ar.reset(hoff)
        fsb = [ar.alloc([128, 1024], F32) for _ in range(2)]
        fTs = [ar.alloc([128, 8, 256], F32) for _ in range(2)]
        identf = ar.alloc([128, 128], F32)
        P.dma("sp", identf, dr["identf"], writes=["identf"])
        dcnt = 0
        Fv = dr["Fscr"].rearrange("(t p) c -> p t c", p=128)
        for cg2 in range(8):
            ws = cg2 % 2
            fts = cg2 % 2
            for hf in range(2):
                P.dma("pool", Wd[ws][:, hf * 22:(hf + 1) * 22, :], w_down[:, hf * 22:(hf + 1) * 22, cg2 * 256:(cg2 + 1) * 256],
                      writes=[f"W{ws}"])
            for ccl in range(2):
                st_ = dcnt % 2
                dcnt += 1
                for th in range(2):
                    bank = 2 * st_ + th
                    for fc in range(NFC):
                        P.add("pe", lambda e, o=ps[:, bank, :], l=Wd[ws][:, fc, ccl * 128:(ccl + 1) * 128],
                              r=gT[:, fc, th * 512:(th + 1) * 512], st=(fc == 0), sp=(fc == NFC - 1):
                              e.matmul(o, l, r, start=st, stop=sp), reads=[f"W{ws}", "gT"], writes=[f"ps{bank}"])
                    P.add("act", lambda e, o=fsb[st_][:, th * 512:(th + 1) * 512], i_=ps[:, bank, :]: e.activation(o, i_, AF.Copy),
                          reads=[f"ps{bank}"], writes=[f"fsb{st_}_{th}"])
                for k in range(2):
                    tbk = 4 + 2 * st_ + k
                    for j in range(4):
                        tile = k * 4 + j
                        P.add("pe", lambda e, o=ps[:, tbk, j * 128:(j + 1) * 128], i_=fsb[st_][:, tile * 128:(tile + 1) * 128]:
                              e.transpose(o, i_, identf), reads=[f"fsb{st_}_{k}", "identf"], writes=[f"ps{tbk}"])
                    P.add("dve", lambda e, o=fTs[fts][:, k * 4:(k + 1) * 4, ccl * 128:(ccl + 1) * 128],
                          i_=ps[:, tbk, :].rearrange("p (a b) -> p a b", b=128): e.tensor_copy(o, i_),
                          reads=[f"ps{tbk}"], writes=[f"fTs{fts}"])
            P.dma("sp", Fv[:, B * 8:(B + 1) * 8, cg2 * 256:(cg2 + 1) * 256], fTs[fts], reads=[f"fTs{fts}"], writes=["Fscr"])
        P.emit()


def phase_E(C):
    nc, P, ar, ps, dr = C["nc"], C["P"], C["arena"], C["ps"], C["dram"]
    ar.reset()
    lng = ar.alloc([128, D], F32)
    lnb = ar.alloc([128, D], F32)
    hs = [ar.alloc([128, D], F32) for _ in range(2)]
    y = [ar.alloc([128, D], F32) for _ in range(2)]
    T = _ln_temps(ar, P)
    P.dma("sp", lng, dr["lnp"][2].partition_broadcast(128), writes=["lng"])
    P.dma("sp", lnb, dr["lnp"][3].partition_broadcast(128), writes=["lnb"])
    for t in range(16):
        sl = t % 2
        c0 = t * 128
        P.dma("sp", hs[sl], dr["H1scr"][c0:c0 + 128, :], writes=[f"hs{sl}"])
        P.dma("sp", y[sl], dr["Fscr"][c0:c0 + 128, :], writes=[f"y{sl}"])
        P.add("dve", lambda e, o=y[sl], a=hs[sl]: e.scalar_tensor_tensor(o, a, ALPHA, o, ALU.mult, ALU.add),
              reads=[f"hs{sl}", f"y{sl}"], writes=[f"y{sl}"])
        _layernorm(P, y[sl], 128, sl, T, lng, lnb, f"y{sl}")
        P.dma("sp", dr["out"][c0:c0 + 128, :], y[sl], reads=[f"y{sl}"], writes=["out"])
    P.emit()


def _rope_tab(pos):
    pos = pos.astype(np.float32)
    out = np.zeros((pos.shape[0], 96), np.float32)
    inv_m = (1.0 / (np.float32(500000.0) ** (np.arange(0, 32, 2, dtype=np.float32) / np.float32(32)))).astype(np.float32)
    inv_d = (1.0 / (np.float32(500000.0) ** (np.arange(0, 16, 2, dtype=np.float32) / np.float32(16)))).astype(np.float32)
    am = (pos[:, None] * inv_m[None, :]).astype(np.float32)
    ad = (pos[:, None] * inv_d[None, :]).astype(np.float32)
    cm, sm = np.cos(am).astype(np.float32), np.sin(am).astype(np.float32)
    cd, sd = np.cos(ad).astype(np.float32), np.sin(ad).astype(np.float32)
    out[:, 0:16] = cm
    out[:, 16:32] = cm
    out[:, 32:48] = -sm
    out[:, 48:64] = sm
    out[:, 64:72] = cd
    out[:, 72:80] = cd
    out[:, 80:88] = -sd
    out[:, 88:96] = sd
    return out


def _own_positions(p):
    main = np.concatenate([np.arange((2 * j + p) * 128, (2 * j + p + 1) * 128) for j in range(16)])
    halo = []
    for j in range(16):
        s0 = (2 * j + p) * 128
        halo += [s0 - 2, s0 - 1]
    halo = np.array(halo)
    valid = halo >= 0
    halo_c = np.where(valid, halo, 0)
    return main, halo_c, valid


def _tile_layout(a, ntile):
    n = a.shape[0]
    pad = ntile * 128 - n
    if pad:
        a = np.concatenate([a, np.zeros((pad,) + a.shape[1:], a.dtype)], axis=0)
    a = a.reshape((ntile, 128) + a.shape[1:])
    return np.ascontiguousarray(np.swapaxes(a, 0, 1))


def _core_consts(p):
    main, halo, valid = _own_positions(p)
    own = np.concatenate([main, halo])
    c = {}
    c["tabk"] = _tile_layout(_rope_tab(np.arange(S)), 32)
    c["tabq"] = _tile_layout(_rope_tab(own), 17)
    k = np.arange(128)[:, None]
    q = np.arange(128)[None, :]
    cm = np.zeros((128, 2, 128), np.float32)
    for i in range(2):
        cm[:, i, :] = np.where((i - p) * 128 + k > q, -BIG, 0.0)
    c["cmask"] = cm.astype(ml_dtypes.bfloat16)
    hm = np.zeros((128, 32, 32), np.float32)
    for kt in range(32):
        hm[:, kt, :] = np.where(kt * 128 + k > halo[None, :], -BIG, 0.0)
    c["hmask"] = hm.astype(ml_dtypes.bfloat16)
    c["identb"] = np.eye(128, dtype=np.float32).astype(ml_dtypes.bfloat16)
    c["identf"] = np.eye(128, dtype=np.float32)
    es = np.zeros((16, 16, 128), np.float32)
    for n in range(16):
        es[n, n, :] = 1.0
    c["esel"] = es.reshape(16, 2048).astype(ml_dtypes.bfloat16)
    qblk = own // 256
    n = np.arange(16)[None, :]
    pastb = np.where(n < qblk[:, None], 0.0, -1e30).astype(np.float32)
    ownb = np.where(n == qblk[:, None], 0.0, -BIG).astype(np.float32)
    c["pastb"] = _tile_layout(pastb, 17)
    c["ownb"] = _tile_layout(ownb, 17)
    hv = np.repeat(valid.astype(np.float32)[None, :], 128, axis=0)
    c["hvalid"] = np.ascontiguousarray(hv)
    return c, own


def build_program(phases, debug):
    nc = bass.Bass("TRN2", target_bir_lowering=False)
    dr = {}

    def din(name, shape, dt=F32):
        dr[name] = nc.dram_tensor(name, list(shape), dt, kind="ExternalInput").ap()

    def dscr(name, shape, dt):
        kind = "ExternalOutput" if name in debug else "Internal"
        dr[name] = nc.dram_tensor(name, list(shape), dt, kind=kind).ap()

    din("xT", [D, S]); din("xTq", [D, QW]); din("xq", [QW, D])
    din("w_in", [D, 6144]); din("w_out", [D, D]); din("w_up", [D, 2 * FF]); din("w_down", [FF, D])
    din("tabk", [128, 32, 96]); din("tabq", [128, 17, 96])
    din("cmask", [128, 2, 128], BF16); din("hmask", [128, 32, 32], BF16)
    din("identb", [128, 128], BF16); din("identf", [128, 128]); din("esel", [16, 2048], BF16)
    din("pastb", [128, 17, 16]); din("ownb", [128, 17, 16]); din("hvalid", [128, 32])
    din("lnp", [4, D]); din("cwb", [128, NFC, 4]); din("lamp", [4, 64]); din("subg", [128])
    dscr("KTscr", [16, 128, S], BF16)
    dscr("Vscr", [S, D], BF16)
    dscr("QTscr", [16, 128, QW], BF16)
    dscr("ATscr", [D, QW], BF16)
    dscr("H1scr", [QW, D], F32)
    dscr("H1Tscr", [D, QW], BF16)
    dscr("Fscr", [2048, D], F32)
    dr["out"] = nc.dram_tensor("out", [2048, D], F32, kind="ExternalOutput").ap()

    from contextlib import ExitStack
    with ExitStack() as es:
        ARENA = 200 * 1024
        abuf = es.enter_context(nc.sbuf_tensor("arena", [128, ARENA], U8))
        ps = es.enter_context(nc.psum_tensor("psa", [128, 8, 512], F32))
        esem = {e: es.enter_context(nc.semaphore(f"s_{e}")) for e in ("pe", "act", "dve", "pool")}
        rings = {q: [es.enter_context(nc.semaphore(f"r_{q}{i}")) for i in range(RING)] for q in ("sp", "pool")}
        P = Prog(nc, esem, rings)
        C = dict(nc=nc, P=P, arena=Arena(abuf, ARENA), ps=ps, dram=dr)
        for ph in phases:
            PHASES[ph](C)
    return nc


PHASES = {"A": phase_A, "B": phase_B, "C": phase_C, "D": phase_D, "E": phase_E}


def kernel(**inputs):
    phases = os.environ.get("MK_PHASES", "ABCDE")
    debug = [s for s in os.environ.get("MK_DEBUG", "").split(",") if s]
    x = np.asarray(inputs["x"], np.float32)
    shared = {
        "w_in": np.ascontiguousarray(np.asarray(inputs["w_in"], np.float32)[0]),
        "w_out": np.ascontiguousarray(np.asarray(inputs["w_out"], np.float32)[0]),
        "w_up": np.ascontiguousarray(np.asarray(inputs["w_up"], np.float32)[0]),
        "w_down": np.ascontiguousarray(np.asarray(inputs["w_down"], np.float32)[0]),
    }
    lnp = np.stack([np.asarray(inputs[k], np.float32)[0] for k in ("ln1_g", "ln1_b", "ln2_g", "ln2_b")])
    shared["lnp"] = np.ascontiguousarray(lnp)
    cw = np.asarray(inputs["conv_w"], np.float32)[0]
    cb = np.asarray(inputs["conv_b"], np.float32)[0]
    cwb = np.concatenate([cw, cb[None, :]], axis=0)
    cwb = cwb.reshape(4, NFC, 128).transpose(2, 1, 0)
    shared["cwb"] = np.ascontiguousarray(cwb)
    shared["lamp"] = np.ascontiguousarray(np.stack(
        [np.asarray(inputs[k], np.float32)[0] for k in ("lambda_q1", "lambda_k1", "lambda_q2", "lambda_k2")]))
    shared["subg"] = np.ascontiguousarray(np.asarray(inputs["subln_g"], np.float32)[0])

    consts = [_core_consts(p) for p in range(2)]
    in_maps = []
    for core in range(8):
        b, p = core // 2, core % 2
        cc, own = consts[p]
        xb = x[b]
        m = dict(shared)
        m.update(cc)
        m["xT"] = np.ascontiguousarray(xb.T)
        xq = xb[own]
        m["xq"] = np.ascontiguousarray(xq)
        m["xTq"] = np.ascontiguousarray(xq.T)
        in_maps.append(m)
    nc = build_program(phases, debug)
    res = run_bass_kernel_spmd(nc, in_maps, core_ids=list(range(8)))
    if debug:
        kernel.debug = [{k: np.asarray(r[k]) for k in debug} for r in res.results]
    out = np.zeros((4, S, D), np.float32)
    for core in range(8):
        b, p = core // 2, core % 2
        o = np.asarray(res.results[core]["out"])
        for j in range(16):
            g = 2 * j + p
            out[b, g * 128:(g + 1) * 128] = o[j * 128:(j + 1) * 128]
    return out
```

```python
import os
import numpy as np
import ml_dtypes
import concourse.bass as bass
import concourse.mybir as mybir
from concourse.bass_utils import run_bass_kernel_spmd

F32 = mybir.dt.float32
BF16 = mybir.dt.bfloat16
U8 = mybir.dt.uint8
ALU = mybir.AluOpType
AF = mybir.ActivationFunctionType
AX = mybir.AxisListType

S = 4096
D = 2048
NT = 32
NQT = 17
QW = 2080
FF = 5632
NFC = 44
BIG = 30720.0
ALPHA = 2.0 ** 0.25
LAM_INIT = 0.2
DIFF_SCALE = 64 ** -0.5
MOBA_SCALE = 128 ** -0.5
LN_EPS = 1e-5
RMS_EPS = 1e-5

ENGS = ("pe", "act", "dve", "pool", "sp")
RING = 8


class Prog:
    def __init__(self, nc, esem, rings):
        self.nc = nc
        self.esem = esem
        self.rings = rings
        self.sigcount = {e: 0 for e in ENGS}
        self.dmacount = {q: 0 for q in rings}
        self.waited = {e: {} for e in ENGS}
        self.reset()

    def reset(self):
        self.ops = []
        self.lastw = {}
        self.readers = {}

    def _deps(self, reads, writes):
        writes = list(writes) + [t for t in reads if t.startswith("ps") and t not in writes]
        deps = {}
        for t in reads:
            w = self.lastw.get(t)
            if w is not None:
                deps[w] = True
        for t in writes:
            w = self.lastw.get(t)
            if w is not None:
                deps.setdefault(w, False)
            for r in self.readers.get(t, ()):
                deps.setdefault(r, False)
        idx = len(self.ops)
        for t in writes:
            self.lastw[t] = idx
            self.readers[t] = []
        for t in reads:
            if t not in writes:
                self.readers.setdefault(t, []).append(idx)
        return deps

    def add(self, eng, fn, reads=(), writes=()):
        deps = self._deps(reads, writes)
        self.ops.append(dict(eng=eng, fn=fn, deps=deps, dma=False, signal=False))

    def dma(self, q, out, in_, reads=(), writes=()):
        deps = self._deps(reads, writes)
        k = self.dmacount[q]
        self.dmacount[q] = k + 1
        self.ops.append(dict(eng=q, fn=None, out=out, in_=in_, deps=deps, dma=True, k=k,
                             sem=self.rings[q][k % RING], val=16 * (k // RING + 1)))

    def emit(self):
        ops = self.ops
        for o in ops:
            need = []
            for pi, raw in o["deps"].items():
                p = ops[pi]
                if p["dma"]:
                    need.append(pi)
                elif (not o["dma"]) and p["eng"] == o["eng"]:
                    if o["eng"] != "pe" and raw:
                        need.append(pi)
                else:
                    need.append(pi)
            o["need"] = need
            for pi in need:
                if not ops[pi]["dma"]:
                    ops[pi]["signal"] = True
        for o in ops:
            if (not o["dma"]) and o["signal"]:
                self.sigcount[o["eng"]] += 1
                o["sem"] = self.esem[o["eng"]]
                o["val"] = self.sigcount[o["eng"]]
        streams = {e: [o for o in ops if o["eng"] == e] for e in ENGS}
        nc = self.nc
        with nc.Block() as block:
            hmap = {"pe": block.tensor, "act": block.scalar, "dve": block.vector,
                    "pool": block.gpsimd, "sp": block.sync}
            for e in ENGS:
                st = streams[e]
                if not st and e not in self.rings:
                    continue

                def body(eh, e=e, st=st):
                    wd = self.waited[e]

                    def wait(sem, val):
                        key = id(sem)
                        if wd.get(key, 0) < val:
                            eh.wait_ge(sem, val)
                            wd[key] = val

                    for o in st:
                        for pi in o["need"]:
                            p = ops[pi]
                            wait(p["sem"], p["val"])
                        if o["dma"]:
                            if o["k"] >= RING:
                                wait(o["sem"], o["val"] - 16)
                            eh.dma_start(out=o["out"], in_=o["in_"]).then_inc(o["sem"], 16)
                        else:
                            ins = o["fn"](eh)
                            if o["signal"]:
                                ins.then_inc(o["sem"], 1)
                    if e in self.rings:
                        k = self.dmacount[e]
                        for slot in range(RING):
                            n = (k - slot + RING - 1) // RING if k > slot else 0
                            if n > 0:
                                wait(self.rings[e][slot], 16 * n)

                hmap[e](body)
        self.reset()


class Arena:
    def __init__(self, buf, size):
        self.buf = buf
        self.size = size
        self.off = 0

    def reset(self, off=0):
        self.off = off

    def alloc(self, shape, dt):
        esz = 2 if dt == BF16 else 4
        n = esz
        for s in shape[1:]:
            n *= s
        n = (n + 63) // 64 * 64
        assert self.off + n <= self.size, ("SBUF arena overflow", self.off, n, self.size)
        v = self.buf[:, self.off:self.off + n].bitcast(dt)
        self.off += n
        cnt = 1
        for s in shape[1:]:
            cnt *= s
        v = v[:, 0:cnt]
        if len(shape) == 3:
            v = v.rearrange("p (a b) -> p a b", a=shape[1])
        elif len(shape) == 4:
            v = v.rearrange("p (a b c) -> p a b c", a=shape[1], b=shape[2])
        return v


def bc(ap, shape):
    return ap.to_broadcast(list(shape))


def phase_A(C):
    nc, P, ar, ps = C["nc"], C["P"], C["arena"], C["ps"]
    dr = C["dram"]
    ar.reset()
    Wb = [ar.alloc([128, 16, 1024], BF16) for _ in range(2)]
    Xb = [ar.alloc([128, 16, 512], BF16) for _ in range(3)]
    tabk = ar.alloc([128, 32, 96], F32)
    tabq = ar.alloc([128, 17, 96], F32)
    KR = [ar.alloc([128, 1024], BF16) for _ in range(2)]
    T1 = [ar.alloc([128, 256], F32) for _ in range(2)]
    T2 = [ar.alloc([128, 256], F32) for _ in range(2)]
    KTs = [ar.alloc([128, 8, 512], BF16) for _ in range(2)]
    Vs = [ar.alloc([128, 4, 1024], BF16) for _ in range(2)]
    identb = ar.alloc([128, 128], BF16)

    P.dma("sp", tabk, dr["tabk"], writes=["tabk"])
    P.dma("sp", tabq, dr["tabq"], writes=["tabq"])
    P.dma("sp", identb, dr["identb"], writes=["identb"])

    w_in = dr["w_in"].rearrange("(kc p) n -> p kc n", p=128)
    xT = dr["xT"].rearrange("(kc p) t -> p kc t", p=128)
    xTq = dr["xTq"].rearrange("(kc p) t -> p kc t", p=128)
    Vscr = dr["Vscr"].rearrange("(kt p) c -> p kt c", p=128)

    passes = [
        ("K", "d", 1024, 0), ("V", None, 2048, 0), ("K", "m", 4096, 8), ("V", None, 5120, 8),
        ("Q", "d", 0, 0), ("Q", "m", 3072, 8),
    ]
    cnt = dict(w=0, x=0, tile=0, kr=0, st=0)
    _sel = os.environ.get("MK_A_PASSES")
    if _sel:
        passes = [passes[int(i)] for i in _sel.split(",")]
    _nb = int(os.environ.get("MK_A_BLOCKS", "99"))
    _norope = bool(os.environ.get("MK_A_NOROPE"))
    for (kind, rk, col0, h0) in passes:
        wslot = cnt["w"] % 2
        cnt["w"] += 1
        for half in range(2):
            P.dma("pool", Wb[wslot][:, :, half * 512:(half + 1) * 512],
                  w_in[:, :, col0 + half * 512: col0 + (half + 1) * 512],
                  writes=[f"Wb{wslot}"])
        if kind == "Q":
            blocks = [(i * 512, 512) for i in range(4)] + [(2048, 32)]
            xsrc, tab = xTq, tabq
        else:
            blocks = [(i * 512, 512) for i in range(8)]
            xsrc, tab = xT, tabk
        for (tok0, ntok) in blocks[:_nb]:
            xslot = cnt["x"] % 3
            cnt["x"] += 1
            P.dma("pool", Xb[xslot][:, :, 0:ntok], xsrc[:, :, tok0:tok0 + ntok], writes=[f"Xb{xslot}"])
            sslot = cnt["st"] % 2
            cnt["st"] += 1
            ntile = (ntok + 127) // 128
            for t in range(ntile):
                rows = min(128, ntok - t * 128)
                gt = (tok0 + t * 128) // 128
                s = cnt["tile"] % 2
                cnt["tile"] += 1
                b0 = 2 * s
                for cg in range(2):
                    for kc in range(16):
                        P.add("pe", lambda e, o=ps[0:rows, b0 + cg, :], l=Xb[xslot][:, kc, t * 128:t * 128 + rows],
                              r=Wb[wslot][:, kc, cg * 512:(cg + 1) * 512], st=(kc == 0), sp=(kc == 15):
                              e.matmul(o, l, r, start=st, stop=sp),
                              reads=[f"Xb{xslot}", f"Wb{wslot}"], writes=[f"ps{b0 + cg}"])
                if kind == "V":
                    for cg in range(2):
                        P.add("act", lambda e, o=Vs[sslot][0:rows, t, cg * 512:(cg + 1) * 512], i=ps[0:rows, b0 + cg, :]:
                              e.activation(o, i, AF.Copy),
                              reads=[f"ps{b0 + cg}"], writes=[f"Vs{sslot}"])
                else:
                    kr = cnt["kr"] % 2
                    cnt["kr"] += 1
                    for cg in range(2):
                        P.add("act", lambda e, o=KR[kr][0:rows, cg * 512:(cg + 1) * 512], i=ps[0:rows, b0 + cg, :]:
                              e.activation(o, i, AF.Copy),
                              reads=[f"ps{b0 + cg}"], writes=[f"KR{kr}"])
                    if rk == "m":
                        U, W_, R_ = 8, 128, 32
                        cofs, sofs = 0, 32
                    else:
                        U, W_, R_ = 16, 64, 16
                        cofs, sofs = 64, 80
                    H_ = R_ // 2
                    UB = U // 2
                    for bk in range(2):
                        psv = ps[0:rows, b0 + bk, :].rearrange("p (u d) -> p u d", d=W_)
                        t1 = T1[kr][0:rows, bk * UB * R_:(bk + 1) * UB * R_].rearrange("p (u d) -> p u d", d=R_)
                        t2 = T2[kr][0:rows, bk * UB * R_:(bk + 1) * UB * R_].rearrange("p (u d) -> p u d", d=R_)
                        krv = KR[kr][0:rows, bk * 512:(bk + 1) * 512].rearrange("p (u d) -> p u d", d=W_)
                        ctab = tab[0:rows, gt, cofs:cofs + R_].unsqueeze(1)
                        stab_a = tab[0:rows, gt, sofs:sofs + H_].unsqueeze(1)
                        stab_b = tab[0:rows, gt, sofs + H_:sofs + R_].unsqueeze(1)
                        rd = [f"ps{b0 + bk}", "tabk", "tabq"]
                        if _norope:
                            continue
                        P.add("dve", lambda e, o=t1, a=psv[:, :, 0:R_], b_=bc(ctab, [rows, UB, R_]):
                              e.tensor_tensor(o, a, b_, ALU.mult), reads=rd, writes=[f"T1{kr}{bk}"])
                        P.add("dve", lambda e, o=t2[:, :, 0:H_], a=psv[:, :, H_:R_], b_=bc(stab_a, [rows, UB, H_]):
                              e.tensor_tensor(o, a, b_, ALU.mult), reads=rd, writes=[f"T2a{kr}{bk}"])
                        P.add("dve", lambda e, o=t2[:, :, H_:R_], a=psv[:, :, 0:H_], b_=bc(stab_b, [rows, UB, H_]):
                              e.tensor_tensor(o, a, b_, ALU.mult), reads=rd, writes=[f"T2b{kr}{bk}"])
                        P.add("dve", lambda e, o=krv[:, :, 0:R_], a=t1, b_=t2:
                              e.tensor_tensor(o, a, b_, ALU.add),
                              reads=[f"T1{kr}{bk}", f"T2a{kr}{bk}", f"T2b{kr}{bk}"], writes=[f"KR{kr}"])
                    tb = 4 + (cnt["kr"] % 2)
                    psT = ps[:, tb, :].bitcast(BF16).rearrange("p (c t) -> p c t", c=8)
                    for c in range(8):
                        P.add("pe", lambda e, o=psT[:, c, 0:rows], i=KR[kr][0:rows, c * 128:(c + 1) * 128],
                              idn=identb[0:rows, 0:rows]: e.transpose(o, i, idn),
                              reads=[f"KR{kr}", "identb"], writes=[f"ps{tb}"])
                    P.add("dve", lambda e, o=KTs[sslot][:, :, t * 128:t * 128 + rows], i=psT[:, :, 0:rows]:
                          e.tensor_copy(o, i), reads=[f"ps{tb}"], writes=[f"KTs{sslot}"])
            if kind == "V":
                t0 = tok0 // 128
                P.dma("sp", Vscr[:, t0:t0 + 4, h0 * 128:h0 * 128 + 1024], Vs[sslot][:, :, :],
                      reads=[f"Vs{sslot}"], writes=["Vscr"])
            elif kind == "K":
                P.dma("sp", dr["KTscr"][h0:h0 + 8, :, tok0:tok0 + ntok].rearrange("h d t -> d h t"),
                      KTs[sslot][:, :, 0:ntok], reads=[f"KTs{sslot}"], writes=["KTscr"])
            else:
                P.dma("sp", dr["QTscr"][h0:h0 + 8, :, tok0:tok0 + ntok].rearrange("h d t -> d h t"),
                      KTs[sslot][:, :, 0:ntok], reads=[f"KTs{sslot}"], writes=["QTscr"])
    P.emit()


def phase_B(C):
    nc, P, ar, ps, dr = C["nc"], C["P"], C["arena"], C["ps"], C["dram"]
    ar.reset()
    KTh = [ar.alloc([128, S], BF16) for _ in range(2)]
    Vh = [ar.alloc([128, 32, 129], BF16) for _ in range(2)]
    QTh = [ar.alloc([128, QW], BF16) for _ in range(2)]
    NPT = 6
    pT = [ar.alloc([128, 4, 128], BF16) for _ in range(NPT)]
    negT = [ar.alloc([128, QW], BF16) for _ in range(2)]
    cmask = ar.alloc([128, 2, 128], BF16)
    hmask = ar.alloc([128, 32, 32], BF16)
    identb = ar.alloc([128, 128], BF16)
    esel = ar.alloc([128, 2048], BF16)
    pastb = ar.alloc([128, 17, 16], F32)
    ownb = ar.alloc([128, 17, 16], F32)
    kmf = ar.alloc([128, 16], F32)
    kmT = [ar.alloc([128, 16], BF16) for _ in range(2)]
    gm = ar.alloc([128, 17, 16], F32)
    top8 = ar.alloc([128, 17, 8], F32)
    thr = ar.alloc([128, 17], F32)
    sel = ar.alloc([128, 17, 16], F32)
    negm = ar.alloc([128, 17, 16], BF16)
    lamp = ar.alloc([128, 4, 64], F32)
    lprod = ar.alloc([128, 2, 64], F32)
    lsum = ar.alloc([128, 2], F32)
    lexp = ar.alloc([128, 2], F32)
    ltmp = ar.alloc([128, 1], F32)
    lamneg = ar.alloc([128, 1], F32)
    gsub = ar.alloc([128, 128], F32)
    mhalf = ar.alloc([128, 1], F32)
    rec = [ar.alloc([128, 2], F32) for _ in range(2)]
    rec1n = [ar.alloc([128, 1], F32) for _ in range(2)]
    Osb = [ar.alloc([128, 128], F32) for _ in range(2)]
    Ao = [ar.alloc([128, 128], F32) for _ in range(2)]
    junk = ar.alloc([128, 128], F32)
    ss = [ar.alloc([128, 1], F32) for _ in range(2)]
    ms = [ar.alloc([128, 1], F32) for _ in range(2)]
    rstd = [ar.alloc([128, 1], F32) for _ in range(2)]
    Ab = [ar.alloc([128, 128], BF16) for _ in range(2)]
    ATs = [ar.alloc([128, QW], BF16) for _ in range(2)]

    P.dma("sp", cmask, dr["cmask"], writes=["cmask"])
    P.dma("sp", hmask, dr["hmask"], writes=["hmask"])
    P.dma("sp", identb, dr["identb"], writes=["identb"])
    P.dma("sp", esel[0:16, :], dr["esel"], writes=["esel"])
    P.dma("sp", pastb, dr["pastb"], writes=["pastb"])
    P.dma("sp", ownb, dr["ownb"], writes=["ownb"])
    P.dma("sp", lamp, dr["lamp"].partition_broadcast(128), writes=["lamp"])
    P.dma("sp", gsub, dr["subg"].partition_broadcast(128), writes=["gsub"])
    for i in range(2):
        P.add("dve", lambda e, o=Vh[i][:, :, 128:129]: e.memset(o, 1.0), writes=[f"Vh{i}"])
    P.add("dve", lambda e: e.memset(mhalf, -0.5), writes=["mhalf"])
    lampv = lamp.rearrange("p (a b) d -> p a b d", b=2)
    P.add("dve", lambda e: e.tensor_tensor(lprod, lampv[:, :, 0, :], lampv[:, :, 1, :], ALU.mult),
          reads=["lamp"], writes=["lprod"])
    P.add("dve", lambda e: e.tensor_reduce(lsum, lprod, AX.X, ALU.add), reads=["lprod"], writes=["lsum"])
    P.add("act", lambda e: e.activation(lexp, lsum, AF.Exp), reads=["lsum"], writes=["lexp"])
    P.add("dve", lambda e: e.tensor_tensor(ltmp, lexp[:, 0:1], lexp[:, 1:2], ALU.subtract),
          reads=["lexp"], writes=["ltmp"])
    P.add("dve", lambda e: e.tensor_scalar(lamneg, ltmp, LAM_INIT, -1.0, ALU.add, ALU.mult),
          reads=["ltmp"], writes=["lamneg"])
    P.add("dve", lambda e: e.tensor_scalar(gsub, gsub, 1.0 - LAM_INIT, None, ALU.mult),
          reads=["gsub"], writes=["gsub"])

    Vscr = dr["Vscr"].rearrange("(kt p) c -> p kt c", p=128)
    cnt = dict(batch=0, pt=0, ep=0, tile=0)
    heads = C.get("heads", list(range(16)))
    def loads(h_):
        hb_ = h_ % 2
        P.dma("sp", KTh[hb_], dr["KTscr"][h_], writes=[f"KTh{hb_}"])
        P.dma("sp", QTh[hb_], dr["QTscr"][h_], writes=[f"QTh{hb_}"])
        for hf in range(2):
            P.dma("sp", Vh[hb_][:, hf * 16:(hf + 1) * 16, 0:128], Vscr[:, hf * 16:(hf + 1) * 16, h_ * 128:(h_ + 1) * 128],
                  writes=[f"Vh{hb_}"])

    loads(heads[0])
    for hidx, h in enumerate(heads):
        hb = h % 2
        moba = h >= 8
        if hidx + 1 < len(heads):
            loads(heads[hidx + 1])

        def qinfo(t):
            if t < 16:
                return 128, t * 128, 2 * t + 2
            return 32, 2048, 32

        if moba:
            P.add("dve", lambda e, i=KTh[hb].rearrange("p (n k) -> p n k", k=256): e.tensor_reduce(kmf, i, AX.X, ALU.add),
                  reads=[f"KTh{hb}"], writes=["kmf"])
            P.add("dve", lambda e, o=kmT[hb]: e.tensor_scalar(o, kmf, 1.0 / 256.0, None, ALU.mult),
                  reads=["kmf"], writes=[f"kmT{hb}"])
            psG = ps[:, 6, 0:272].rearrange("p (t n) -> p t n", n=16)
            for t in range(17):
                nq, c0, _ = qinfo(t)
                P.add("pe", lambda e, o=psG[0:nq, t, :], l=QTh[hb][:, c0:c0 + nq], r=kmT[hb], st=(t == 0), sp=(t == 16):
                      e.matmul(o, l, r, start=st, stop=sp, skip_group_check=True),
                      reads=[f"QTh{hb}", f"kmT{hb}"], writes=["ps6"])
            P.add("dve", lambda e: e.tensor_tensor(gm[:, 0:16, :], psG[:, 0:16, :], pastb[:, 0:16, :], ALU.add),
                  reads=["ps6", "pastb"], writes=["gm"])
            P.add("dve", lambda e: e.tensor_tensor(gm[0:32, 16, :], psG[0:32, 16, :], pastb[0:32, 16, :], ALU.add),
                  reads=["ps6", "pastb"], writes=["gmh"])
            for t in range(17):
                nq = 128 if t < 16 else 32
                P.add("dve", lambda e, o=top8[0:nq, t, :], i=gm[0:nq, t, :]: e.max(o, i),
                      reads=["gm", "gmh"], writes=[f"top8_{t}"])
            t8 = [f"top8_{t}" for t in range(17)]
            P.add("dve", lambda e: e.tensor_scalar(thr[:, 0:16], top8[:, 0:16, 2], -1e29, None, ALU.max),
                  reads=t8, writes=["thr"])
            P.add("dve", lambda e: e.tensor_scalar(thr[0:32, 16:17], top8[0:32, 16, 2:3], -1e29, None, ALU.max),
                  reads=t8, writes=["thrh"])
            P.add("dve", lambda e: e.tensor_tensor(sel[:, 0:16, :], gm[:, 0:16, :],
                                                   bc(thr[:, 0:16].unsqueeze(2), [128, 16, 16]), ALU.is_ge),
                  reads=["gm", "thr"], writes=["sel"])
            P.add("dve", lambda e: e.tensor_tensor(sel[0:32, 16, :], gm[0:32, 16, :],
                                                   bc(thr[0:32, 16:17], [32, 16]), ALU.is_ge),
                  reads=["gmh", "thrh"], writes=["selh"])
            P.add("dve", lambda e: e.scalar_tensor_tensor(negm[:, 0:16, :], sel[:, 0:16, :], BIG, ownb[:, 0:16, :],
                                                          ALU.mult, ALU.add),
                  reads=["sel", "ownb"], writes=["negm"])
            P.add("dve", lambda e: e.scalar_tensor_tensor(negm[0:32, 16, :], sel[0:32, 16, :], BIG, ownb[0:32, 16, :],
                                                          ALU.mult, ALU.add),
                  reads=["selh", "ownb"], writes=["negmh"])
            psNT = ps[:, 6, :].bitcast(BF16)
            for g, tiles in enumerate([list(range(0, 8)), list(range(8, 16)), [16]]):
                for i, t in enumerate(tiles):
                    nq = 128 if t < 16 else 32
                    P.add("pe", lambda e, o=psNT[0:16, i * 128:i * 128 + nq], i_=negm[0:nq, t, :], idn=identb[0:nq, 0:nq]:
                          e.transpose(o, i_, idn), reads=["negm", "negmh", "identb"], writes=["ps6"])
                ncols = 1024 if g < 2 else 32
                P.add("dve", lambda e, o=negT[hb][0:16, g * 1024:g * 1024 + ncols], i_=psNT[0:16, 0:ncols]:
                      e.tensor_copy(o, i_), reads=["ps6"], writes=[f"negT{hb}"])

        subs = [0] if moba else [0, 1]
        scale = MOBA_SCALE if moba else DIFF_SCALE
        pending = []
        deferred = []

        def flush_pending():
            for f in pending:
                f()
            pending.clear()

        def flush_deferred():
            for f in deferred:
                f()
            deferred.clear()

        psAT = ps[:, 7, :].bitcast(BF16).rearrange("p (c t) -> p c t", c=8)

        def epilogue(t, nq, c0, ob):
            eb = cnt["ep"] % 2
            cnt["ep"] += 1
            psO = ps[:, ob, 0:258].rearrange("p (s d) -> p s d", d=129)
            pt = f"ps{ob}"
            if moba:
                P.add("dve", lambda e: e.reciprocal(rec[eb][0:nq, 0:1], psO[0:nq, 0, 128:129]),
                      reads=[pt], writes=[f"rec{eb}"])
                P.add("dve", lambda e: e.tensor_scalar(Ab[eb][0:nq, :], psO[0:nq, 0, 0:128], rec[eb][0:nq, 0:1], None, ALU.mult),
                      reads=[pt, f"rec{eb}"], writes=[f"Ab{eb}"])
            else:
                P.add("dve", lambda e: e.reciprocal(rec[eb][0:nq, 0:2], psO[0:nq, :, 128]),
                      reads=[pt], writes=[f"rec{eb}"])
                P.add("dve", lambda e: e.tensor_scalar(rec1n[eb][0:nq, :], rec[eb][0:nq, 1:2], lamneg[0:nq, 0:1], None, ALU.mult),
                      reads=[f"rec{eb}", "lamneg"], writes=[f"rec1n{eb}"])
                P.add("dve", lambda e: e.tensor_scalar(Osb[eb][0:nq, :], psO[0:nq, 0, 0:128], rec[eb][0:nq, 0:1], None, ALU.mult),
                      reads=[pt, f"rec{eb}"], writes=[f"Osb{eb}"])
                P.add("dve", lambda e: e.scalar_tensor_tensor(Ao[eb][0:nq, :], psO[0:nq, 1, 0:128], rec1n[eb][0:nq, 0:1],
                                                              Osb[eb][0:nq, :], ALU.mult, ALU.add),
                      reads=[pt, f"rec1n{eb}", f"Osb{eb}"], writes=[f"Ao{eb}"])
                P.add("dve", lambda e: e.scalar_tensor_tensor(junk[0:nq, :], Ao[eb][0:nq, :], 1.0, Ao[eb][0:nq, :],
                                                              ALU.mult, ALU.mult, accum_out=ss[eb][0:nq, :]),
                      reads=[f"Ao{eb}"], writes=["junk", f"ss{eb}"])
                P.add("dve", lambda e: e.tensor_scalar(ms[eb][0:nq, :], ss[eb][0:nq, :], 1.0 / 128.0, RMS_EPS, ALU.mult, ALU.add),
                      reads=[f"ss{eb}"], writes=[f"ms{eb}"])
                P.add("pool", lambda e: e.tensor_tensor(rstd[eb][0:nq, :], ms[eb][0:nq, :], mhalf[0:nq, :], ALU.pow),
                      reads=[f"ms{eb}", "mhalf"], writes=[f"rstd{eb}"])
                P.add("dve", lambda e: e.scalar_tensor_tensor(Ab[eb][0:nq, :], Ao[eb][0:nq, :], rstd[eb][0:nq, 0:1],
                                                              gsub[0:nq, :], ALU.mult, ALU.mult),
                      reads=[f"Ao{eb}", f"rstd{eb}", "gsub"], writes=[f"Ab{eb}"])

            def tr():
                P.add("pe", lambda e, o=psAT[:, t % 8, 0:nq], i_=Ab[eb][0:nq, :], idn=identb[0:nq, 0:nq]:
                      e.transpose(o, i_, idn), reads=[f"Ab{eb}", "identb"], writes=["ps7"])
                if t in (7, 15, 16):
                    g0 = (t // 8) * 1024
                    ncols = 1024 if t < 16 else 32
                    P.add("act", lambda e, o=ATs[hb][:, g0:g0 + ncols],
                          i_=ps[:, 7, :].bitcast(BF16)[:, 0:ncols]: e.activation(o, i_, AF.Copy),
                          reads=["ps7"], writes=[f"ATs{hb}"])
            deferred.append(tr)

        for t in range(17):
            nq, c0, nkt = qinfo(t)
            ob = 4 + (cnt["tile"] % 2)
            cnt["tile"] += 1
            psO = ps[:, ob, 0:258].rearrange("p (s d) -> p s d", d=129)
            batches = [list(range(i, min(i + 4, nkt))) for i in range(0, nkt, 4)]
            for bi, batch in enumerate(batches):
                sset = cnt["batch"] % 2
                cnt["batch"] += 1
                nb = len(batch)
                psS = [ps[:, 2 * sset + s_, :].rearrange("p (i q) -> p i q", q=128) for s_ in range(2)]
                for i, kt in enumerate(batch):
                    for s_ in subs:
                        rs0 = 0 if moba else 64 * s_
                        rs1 = 128 if moba else 64 * s_ + 64
                        extra = moba or (t < 16 and kt >= 2 * t) or t == 16
                        P.add("pe", lambda e, o=psS[s_][:, i, 0:nq], l=KTh[hb][rs0:rs1, kt * 128:(kt + 1) * 128],
                              r=QTh[hb][rs0:rs1, c0:c0 + nq], st=(i == 0), sp=(not extra):
                              e.matmul(o, l, r, start=st, stop=sp, skip_group_check=True),
                              reads=[f"KTh{hb}", f"QTh{hb}"], writes=[f"ps{2 * sset + s_}"])
                    causal = (t < 16 and kt >= 2 * t) or t == 16
                    if moba:
                        blk = kt // 2
                        P.add("pe", lambda e, o=psS[0][:, i, 0:nq], l=esel[0:16, blk * 128:(blk + 1) * 128],
                              r=negT[hb][0:16, c0:c0 + nq], sp=(not causal):
                              e.matmul(o, l, r, start=False, stop=sp, skip_group_check=True),
                              reads=["esel", f"negT{hb}"], writes=[f"ps{2 * sset}"])
                    if causal:
                        mk = cmask[:, kt - 2 * t, :] if t < 16 else hmask[:, kt, :]
                        for s_ in subs:
                            P.add("pe", lambda e, o=psS[s_][:, i, 0:nq], r=mk:
                                  e.matmul(o, identb, r, start=False, stop=True, skip_group_check=True),
                                  reads=["identb", "cmask", "hmask"], writes=[f"ps{2 * sset + s_}"])
                slots = {}
                for s_ in subs:
                    pslot = cnt["pt"] % NPT
                    cnt["pt"] += 1
                    slots[s_] = pslot
                    P.add("act", lambda e, o=pT[pslot][:, 0:nb, 0:nq], i_=psS[s_][:, 0:nb, 0:nq], sc=scale:
                          e.activation(o, i_, AF.Exp, scale=sc),
                          reads=[f"ps{2 * sset + s_}"], writes=[f"pT{pslot}"])
                flush_pending()
                flush_deferred()

                def pv(batch=batch, slots=dict(slots), nq=nq, nkt=nkt, psO=psO, ob=ob, t=t, c0=c0, last=(bi == len(batches) - 1)):
                    for i, kt in enumerate(batch):
                        for s_ in subs:
                            P.add("pe", lambda e, o=psO[0:nq, s_, :], l=pT[slots[s_]][:, i, 0:nq], r=Vh[hb][:, kt, :],
                                  st=(kt == 0 and s_ == 0), sp=(kt == nkt - 1):
                                  e.matmul(o, l, r, start=st, stop=sp, skip_group_check=True),
                                  reads=[f"pT{slots[s_]}", f"Vh{hb}"], writes=[f"ps{ob}"])
                    if last:
                        epilogue(t, nq, c0, ob)
                pending.append(pv)
        flush_pending()
        flush_deferred()
        P.dma("sp", dr["ATscr"][h * 128:(h + 1) * 128, :], ATs[hb], reads=[f"ATs{hb}"], writes=["ATscr"])
    P.emit()


def _layernorm(P, y, rows, slot, T, lng, lnb, ytok):
    stats, mv, veps, rstd, nmr, mhalf = T["stats"][slot], T["mv"][slot], T["veps"][slot], T["rstd"][slot], T["nmr"][slot], T["mhalf"]
    for c in range(4):
        P.add("dve", lambda e, o=stats[0:rows, c, :], i_=y[0:rows, c * 512:(c + 1) * 512]: e.bn_stats(o, i_),
              reads=[ytok], writes=[f"stats{slot}_{c}"])
    P.add("dve", lambda e: e.bn_aggr(mv[0:rows, :], stats[0:rows, :, :].rearrange("p a b -> p (a b)")),
          reads=[f"stats{slot}_{c}" for c in range(4)], writes=[f"mv{slot}"])
    P.add("dve", lambda e: e.tensor_scalar(veps[0:rows, :], mv[0:rows, 1:2], LN_EPS, None, ALU.add),
          reads=[f"mv{slot}"], writes=[f"veps{slot}"])
    P.add("pool", lambda e: e.tensor_tensor(rstd[0:rows, :], veps[0:rows, :], mhalf[0:rows, :], ALU.pow),
          reads=[f"veps{slot}", "mhalf"], writes=[f"rstd{slot}"])
    P.add("dve", lambda e: e.tensor_scalar(nmr[0:rows, :], mv[0:rows, 0:1], rstd[0:rows, 0:1], -1.0, ALU.mult, ALU.mult),
          reads=[f"mv{slot}", f"rstd{slot}"], writes=[f"nmr{slot}"])
    P.add("act", lambda e: e.activation(y[0:rows, :], y[0:rows, :], AF.Identity, bias=nmr[0:rows, 0:1], scale=rstd[0:rows, 0:1]),
          reads=[ytok, f"nmr{slot}", f"rstd{slot}"], writes=[ytok])
    P.add("dve", lambda e: e.tensor_tensor(y[0:rows, :], y[0:rows, :], lng[0:rows, :], ALU.mult),
          reads=[ytok, "lng"], writes=[ytok])
    P.add("pool", lambda e: e.tensor_tensor(y[0:rows, :], y[0:rows, :], lnb[0:rows, :], ALU.add),
          reads=[ytok, "lnb"], writes=[ytok])


def _ln_temps(ar, P):
    T = dict(stats=[ar.alloc([128, 4, 6], F32) for _ in range(2)], mv=[ar.alloc([128, 2], F32) for _ in range(2)],
             veps=[ar.alloc([128, 1], F32) for _ in range(2)], rstd=[ar.alloc([128, 1], F32) for _ in range(2)],
             nmr=[ar.alloc([128, 1], F32) for _ in range(2)], mhalf=ar.alloc([128, 1], F32))
    P.add("dve", lambda e: e.memset(T["mhalf"], -0.5), writes=["mhalf"])
    return T


def phase_C(C):
    nc, P, ar, ps, dr = C["nc"], C["P"], C["arena"], C["ps"], C["dram"]
    ar.reset()
    Wo = ar.alloc([128, 16, D], BF16)
    ATb = ar.alloc([128, 16, QW], BF16)
    lng = ar.alloc([128, D], F32)
    lnb = ar.alloc([128, D], F32)
    xs = [ar.alloc([128, D], F32) for _ in range(2)]
    y = [ar.alloc([128, D], F32) for _ in range(2)]
    hb16 = [ar.alloc([128, D], BF16) for _ in range(2)]
    hTs = [ar.alloc([128, 16, 128], BF16) for _ in range(2)]
    identb = ar.alloc([128, 128], BF16)
    T = _ln_temps(ar, P)
    P.dma("sp", identb, dr["identb"], writes=["identb"])
    P.dma("sp", lng, dr["lnp"][0].partition_broadcast(128), writes=["lng"])
    P.dma("sp", lnb, dr["lnp"][1].partition_broadcast(128), writes=["lnb"])
    w_out = dr["w_out"].rearrange("(kc p) n -> p kc n", p=128)
    for cg in range(4):
        P.dma("pool", Wo[:, :, cg * 512:(cg + 1) * 512], w_out[:, :, cg * 512:(cg + 1) * 512], writes=["Wo"])
    ATv = dr["ATscr"].rearrange("(h p) t -> p h t", p=128)
    for q4 in range(4):
        P.dma("sp", ATb[:, q4 * 4:(q4 + 1) * 4, :], ATv[:, q4 * 4:(q4 + 1) * 4, :], writes=["ATb"])
    H1T = dr["H1Tscr"].rearrange("(kc p) t -> p kc t", p=128)
    for t in range(17):
        rows = 128 if t < 16 else 32
        c0 = t * 128
        sl = t % 2
        P.dma("sp", xs[sl][0:rows, :], dr["xq"][c0:c0 + rows, :], writes=[f"xs{sl}"])
        for cg in range(4):
            bank = cg
            for h in range(16):
                P.add("pe", lambda e, o=ps[0:rows, bank, :], l=ATb[:, h, c0:c0 + rows], r=Wo[:, h, cg * 512:(cg + 1) * 512],
                      st=(h == 0), sp=(h == 15): e.matmul(o, l, r, start=st, stop=sp),
                      reads=["ATb", "Wo"], writes=[f"ps{bank}"])
            P.add("dve", lambda e, o=y[sl][0:rows, cg * 512:(cg + 1) * 512], a=xs[sl][0:rows, cg * 512:(cg + 1) * 512],
                  b_=ps[0:rows, bank, :]: e.scalar_tensor_tensor(o, a, ALPHA, b_, ALU.mult, ALU.add),
                  reads=[f"xs{sl}", f"ps{bank}"], writes=[f"y{sl}"])
        _layernorm(P, y[sl], rows, sl, T, lng, lnb, f"y{sl}")
        P.dma("sp", dr["H1scr"][c0:c0 + rows, :], y[sl][0:rows, :], reads=[f"y{sl}"], writes=["H1scr"])
        P.add("act", lambda e, o=hb16[sl][0:rows, :], i_=y[sl][0:rows, :]: e.activation(o, i_, AF.Copy),
              reads=[f"y{sl}"], writes=[f"hb16{sl}"])
        for half in range(2):
            tb = 4 + 2 * sl + half
            psT = ps[:, tb, :].bitcast(BF16).rearrange("p (c t) -> p c t", c=8)
            for c in range(8):
                cc = half * 8 + c
                P.add("pe", lambda e, o=psT[:, c, 0:rows], i_=hb16[sl][0:rows, cc * 128:(cc + 1) * 128],
                      idn=identb[0:rows, 0:rows]: e.transpose(o, i_, idn),
                      reads=[f"hb16{sl}", "identb"], writes=[f"ps{tb}"])
            P.add("dve", lambda e, o=hTs[sl][:, half * 8:(half + 1) * 8, 0:rows], i_=psT[:, :, 0:rows]: e.tensor_copy(o, i_),
                  reads=[f"ps{tb}"], writes=[f"hTs{sl}"])
        P.dma("sp", H1T[:, :, c0:c0 + rows], hTs[sl][:, :, 0:rows], reads=[f"hTs{sl}"], writes=["H1Tscr"])
    P.emit()


def phase_D(C):
    nc, P, ar, ps, dr = C["nc"], C["P"], C["arena"], C["ps"], C["dram"]
    w_up = dr["w_up"].rearrange("(kc p) n -> p kc n", p=128)
    w_down = dr["w_down"].rearrange("(fc p) n -> p fc n", p=128)
    H1T = dr["H1Tscr"].rearrange("(kc p) t -> p kc t", p=128)
    for B in range(2):
        ar.reset()
        gT = ar.alloc([128, NFC, 1024], BF16)
        woff = ar.off
        Wu = [ar.alloc([128, 16, 1024], BF16) for _ in range(2)]
        hoff = ar.off
        hT = ar.alloc([128, 16, 1040], BF16)
        gsb = [ar.alloc([128, 4, 130], F32) for _ in range(2)]
        ccb = [ar.alloc([128, 4, 128], F32) for _ in range(2)]
        sgb = [ar.alloc([128, 4, 128], F32) for _ in range(2)]
        cwb = ar.alloc([128, NFC, 4], F32)
        hval = ar.alloc([128, 32], F32)
        P.dma("sp", cwb, dr["cwb"], writes=["cwb"])
        P.dma("sp", hval, dr["hvalid"], writes=["hval"])
        for q4 in range(4):
            P.dma("sp", hT[:, q4 * 4:(q4 + 1) * 4, 0:1024], H1T[:, q4 * 4:(q4 + 1) * 4, B * 1024:(B + 1) * 1024], writes=["hT"])
        P.dma("sp", hT[:, :, 1024:1040], H1T[:, :, 2048 + B * 16:2048 + (B + 1) * 16], writes=["hT"])
        ucnt = 0
        for fg in range(11):
            ws = fg % 2
            P.dma("pool", Wu[ws][:, :, 0:512], w_up[:, :, fg * 512:(fg + 1) * 512], writes=[f"W{ws}"])
            P.dma("pool", Wu[ws][:, :, 512:1024], w_up[:, :, FF + fg * 512:FF + (fg + 1) * 512], writes=[f"W{ws}"])
            for fcl in range(4):
                fc = fg * 4 + fcl
                for th in range(2):
                    st_ = ucnt % 2
                    ucnt += 1
                    bg, bv, bh = 2 * st_, 2 * st_ + 1, 4 + st_
                    for kc in range(16):
                        P.add("pe", lambda e, o=ps[:, bg, :], l=Wu[ws][:, kc, fcl * 128:(fcl + 1) * 128],
                              r=hT[:, kc, th * 512:(th + 1) * 512], st=(kc == 0), sp=(kc == 15):
                              e.matmul(o, l, r, start=st, stop=sp), reads=[f"W{ws}", "hT"], writes=[f"ps{bg}"])
                    for kc in range(16):
                        P.add("pe", lambda e, o=ps[:, bh, 0:8], l=Wu[ws][:, kc, fcl * 128:(fcl + 1) * 128],
                              r=hT[:, kc, 1024 + th * 8:1024 + (th + 1) * 8], st=(kc == 0), sp=(kc == 15):
                              e.matmul(o, l, r, start=st, stop=sp), reads=[f"W{ws}", "hT"], writes=[f"ps{bh}"])
                    for kc in range(16):
                        P.add("pe", lambda e, o=ps[:, bv, :], l=Wu[ws][:, kc, 512 + fcl * 128:512 + (fcl + 1) * 128],
                              r=hT[:, kc, th * 512:(th + 1) * 512], st=(kc == 0), sp=(kc == 15):
                              e.matmul(o, l, r, start=st, stop=sp), reads=[f"W{ws}", "hT"], writes=[f"ps{bv}"])
                    g = gsb[st_]
                    cc = ccb[st_]
                    sg = sgb[st_]
                    P.add("act", lambda e, o=g[:, :, 2:130], i_=ps[:, bg, :].rearrange("p (a b) -> p a b", b=128):
                          e.activation(o, i_, AF.Copy), reads=[f"ps{bg}"], writes=[f"gsbm{st_}"])
                    hv = hval[:, B * 16 + th * 8:B * 16 + (th + 1) * 8].rearrange("p (a b) -> p a b", b=2)
                    P.add("dve", lambda e, o=g[:, :, 0:2], a=ps[:, bh, 0:8].rearrange("p (a b) -> p a b", b=2), b_=hv:
                          e.tensor_tensor(o, a, b_, ALU.mult), reads=[f"ps{bh}", "hval"], writes=[f"gsbh{st_}"])
                    gr = [f"gsbm{st_}", f"gsbh{st_}", "cwb"]
                    P.add("dve", lambda e, o=cc, a=g[:, :, 2:130], s1=cwb[:, fc, 2:3], s2=cwb[:, fc, 3:4]:
                          e.tensor_scalar(o, a, s1, s2, ALU.mult, ALU.add), reads=gr, writes=[f"cc{st_}"])
                    P.add("dve", lambda e, o=cc, a=g[:, :, 1:129], s1=cwb[:, fc, 1:2]:
                          e.scalar_tensor_tensor(o, a, s1, o, ALU.mult, ALU.add), reads=gr + [f"cc{st_}"], writes=[f"cc{st_}"])
                    P.add("dve", lambda e, o=cc, a=g[:, :, 0:128], s1=cwb[:, fc, 0:1]:
                          e.scalar_tensor_tensor(o, a, s1, o, ALU.mult, ALU.add), reads=gr + [f"cc{st_}"], writes=[f"cc{st_}"])
                    P.add("act", lambda e, o=sg, i_=cc: e.activation(o, i_, AF.Silu), reads=[f"cc{st_}"], writes=[f"sg{st_}"])
                    P.add("dve", lambda e, o=gT[:, fc, th * 512:(th + 1) * 512].rearrange("p (a b) -> p a b", b=128), a=sg,
                          b_=ps[:, bv, :].rearrange("p (a b) -> p a b", b=128): e.tensor_tensor(o, a, b_, ALU.mult),
                          reads=[f"sg{st_}", f"ps{bv}"], writes=["gT"])
        P.emit()
        ar.reset(woff)
        Wd = []
        for i in range(2):
            Wd.append(ar.alloc([128, NFC, 256], BF16))
            ar.reset(woff + (i + 1) * 32768)
        ar.reset(hoff)
        fsb = [ar.alloc([128, 1024], F32) for _ in range(2)]
        fTs = [ar.alloc([128, 8, 256], F32) for _ in range(2)]
        identf = ar.alloc([128, 128], F32)
        P.dma("sp", identf, dr["identf"], writes=["identf"])
        dcnt = 0
        Fv = dr["Fscr"].rearrange("(t p) c -> p t c", p=128)
        for cg2 in range(8):
            ws = cg2 % 2
            fts = cg2 % 2
            for hf in range(2):
                P.dma("pool", Wd[ws][:, hf * 22:(hf + 1) * 22, :], w_down[:, hf * 22:(hf + 1) * 22, cg2 * 256:(cg2 + 1) * 256],
                      writes=[f"W{ws}"])
            for ccl in range(2):
                st_ = dcnt % 2
                dcnt += 1
                for th in range(2):
                    bank = 2 * st_ + th
                    for fc in range(NFC):
                        P.add("pe", lambda e, o=ps[:, bank, :], l=Wd[ws][:, fc, ccl * 128:(ccl + 1) * 128],
                              r=gT[:, fc, th * 512:(th + 1) * 512], st=(fc == 0), sp=(fc == NFC - 1):
                              e.matmul(o, l, r, start=st, stop=sp), reads=[f"W{ws}", "gT"], writes=[f"ps{bank}"])
                    P.add("act", lambda e, o=fsb[st_][:, th * 512:(th + 1) * 512], i_=ps[:, bank, :]: e.activation(o, i_, AF.Copy),
                          reads=[f"ps{bank}"], writes=[f"fsb{st_}_{th}"])
                for k in range(2):
                    tbk = 4 + 2 * st_ + k
                    for j in range(4):
                        tile = k * 4 + j
                        P.add("pe", lambda e, o=ps[:, tbk, j * 128:(j + 1) * 128], i_=fsb[st_][:, tile * 128:(tile + 1) * 128]:
                              e.transpose(o, i_, identf), reads=[f"fsb{st_}_{k}", "identf"], writes=[f"ps{tbk}"])
                    P.add("dve", lambda e, o=fTs[fts][:, k * 4:(k + 1) * 4, ccl * 128:(ccl + 1) * 128],
                          i_=ps[:, tbk, :].rearrange("p (a b) -> p a b", b=128): e.tensor_copy(o, i_),
                          reads=[f"ps{tbk}"], writes=[f"fTs{fts}"])
            P.dma("sp", Fv[:, B * 8:(B + 1) * 8, cg2 * 256:(cg2 + 1) * 256], fTs[fts], reads=[f"fTs{fts}"], writes=["Fscr"])
        P.emit()


def phase_E(C):
    nc, P, ar, ps, dr = C["nc"], C["P"], C["arena"], C["ps"], C["dram"]
    ar.reset()
    lng = ar.alloc([128, D], F32)
    lnb = ar.alloc([128, D], F32)
    hs = [ar.alloc([128, D], F32) for _ in range(2)]
    y = [ar.alloc([128, D], F32) for _ in range(2)]
    T = _ln_temps(ar, P)
    P.dma("sp", lng, dr["lnp"][2].partition_broadcast(128), writes=["lng"])
    P.dma("sp", lnb, dr["lnp"][3].partition_broadcast(128), writes=["lnb"])
    for t in range(16):
        sl = t % 2
        c0 = t * 128
        P.dma("sp", hs[sl], dr["H1scr"][c0:c0 + 128, :], writes=[f"hs{sl}"])
        P.dma("sp", y[sl], dr["Fscr"][c0:c0 + 128, :], writes=[f"y{sl}"])
        P.add("dve", lambda e, o=y[sl], a=hs[sl]: e.scalar_tensor_tensor(o, a, ALPHA, o, ALU.mult, ALU.add),
              reads=[f"hs{sl}", f"y{sl}"], writes=[f"y{sl}"])
        _layernorm(P, y[sl], 128, sl, T, lng, lnb, f"y{sl}")
        P.dma("sp", dr["out"][c0:c0 + 128, :], y[sl], reads=[f"y{sl}"], writes=["out"])
    P.emit()


def _rope_tab(pos):
    pos = pos.astype(np.float32)
    out = np.zeros((pos.shape[0], 96), np.float32)
    inv_m = (1.0 / (np.float32(500000.0) ** (np.arange(0, 32, 2, dtype=np.float32) / np.float32(32)))).astype(np.float32)
    inv_d = (1.0 / (np.float32(500000.0) ** (np.arange(0, 16, 2, dtype=np.float32) / np.float32(16)))).astype(np.float32)
    am = (pos[:, None] * inv_m[None, :]).astype(np.float32)
    ad = (pos[:, None] * inv_d[None, :]).astype(np.float32)
    cm, sm = np.cos(am).astype(np.float32), np.sin(am).astype(np.float32)
    cd, sd = np.cos(ad).astype(np.float32), np.sin(ad).astype(np.float32)
    out[:, 0:16] = cm
    out[:, 16:32] = cm
    out[:, 32:48] = -sm
    out[:, 48:64] = sm
    out[:, 64:72] = cd
    out[:, 72:80] = cd
    out[:, 80:88] = -sd
    out[:, 88:96] = sd
    return out


def _own_positions(p):
    main = np.concatenate([np.arange((2 * j + p) * 128, (2 * j + p + 1) * 128) for j in range(16)])
    halo = []
    for j in range(16):
        s0 = (2 * j + p) * 128
        halo += [s0 - 2, s0 - 1]
    halo = np.array(halo)
    valid = halo >= 0
    halo_c = np.where(valid, halo, 0)
    return main, halo_c, valid


def _tile_layout(a, ntile):
    n = a.shape[0]
    pad = ntile * 128 - n
    if pad:
        a = np.concatenate([a, np.zeros((pad,) + a.shape[1:], a.dtype)], axis=0)
    a = a.reshape((ntile, 128) + a.shape[1:])
    return np.ascontiguousarray(np.swapaxes(a, 0, 1))


def _core_consts(p):
    main, halo, valid = _own_positions(p)
    own = np.concatenate([main, halo])
    c = {}
    c["tabk"] = _tile_layout(_rope_tab(np.arange(S)), 32)
    c["tabq"] = _tile_layout(_rope_tab(own), 17)
    k = np.arange(128)[:, None]
    q = np.arange(128)[None, :]
    cm = np.zeros((128, 2, 128), np.float32)
    for i in range(2):
        cm[:, i, :] = np.where((i - p) * 128 + k > q, -BIG, 0.0)
    c["cmask"] = cm.astype(ml_dtypes.bfloat16)
    hm = np.zeros((128, 32, 32), np.float32)
    for kt in range(32):
        hm[:, kt, :] = np.where(kt * 128 + k > halo[None, :], -BIG, 0.0)
    c["hmask"] = hm.astype(ml_dtypes.bfloat16)
    c["identb"] = np.eye(128, dtype=np.float32).astype(ml_dtypes.bfloat16)
    c["identf"] = np.eye(128, dtype=np.float32)
    es = np.zeros((16, 16, 128), np.float32)
    for n in range(16):
        es[n, n, :] = 1.0
    c["esel"] = es.reshape(16, 2048).astype(ml_dtypes.bfloat16)
    qblk = own // 256
    n = np.arange(16)[None, :]
    pastb = np.where(n < qblk[:, None], 0.0, -1e30).astype(np.float32)
    ownb = np.where(n == qblk[:, None], 0.0, -BIG).astype(np.float32)
    c["pastb"] = _tile_layout(pastb, 17)
    c["ownb"] = _tile_layout(ownb, 17)
    hv = np.repeat(valid.astype(np.float32)[None, :], 128, axis=0)
    c["hvalid"] = np.ascontiguousarray(hv)
    return c, own


def build_program(phases, debug):
    nc = bass.Bass("TRN2", target_bir_lowering=False)
    dr = {}

    def din(name, shape, dt=F32):
        dr[name] = nc.dram_tensor(name, list(shape), dt, kind="ExternalInput").ap()

    def dscr(name, shape, dt):
        kind = "ExternalOutput" if name in debug else "Internal"
        dr[name] = nc.dram_tensor(name, list(shape), dt, kind=kind).ap()

    din("xT", [D, S]); din("xTq", [D, QW]); din("xq", [QW, D])
    din("w_in", [D, 6144]); din("w_out", [D, D]); din("w_up", [D, 2 * FF]); din("w_down", [FF, D])
    din("tabk", [128, 32, 96]); din("tabq", [128, 17, 96])
    din("cmask", [128, 2, 128], BF16); din("hmask", [128, 32, 32], BF16)
    din("identb", [128, 128], BF16); din("identf", [128, 128]); din("esel", [16, 2048], BF16)
    din("pastb", [128, 17, 16]); din("ownb", [128, 17, 16]); din("hvalid", [128, 32])
    din("lnp", [4, D]); din("cwb", [128, NFC, 4]); din("lamp", [4, 64]); din("subg", [128])
    dscr("KTscr", [16, 128, S], BF16)
    dscr("Vscr", [S, D], BF16)
    dscr("QTscr", [16, 128, QW], BF16)
    dscr("ATscr", [D, QW], BF16)
    dscr("H1scr", [QW, D], F32)
    dscr("H1Tscr", [D, QW], BF16)
    dscr("Fscr", [2048, D], F32)
    dr["out"] = nc.dram_tensor("out", [2048, D], F32, kind="ExternalOutput").ap()

    from contextlib import ExitStack
    with ExitStack() as es:
        ARENA = 200 * 1024
        abuf = es.enter_context(nc.sbuf_tensor("arena", [128, ARENA], U8))
        ps = es.enter_context(nc.psum_tensor("psa", [128, 8, 512], F32))
        esem = {e: es.enter_context(nc.semaphore(f"s_{e}")) for e in ("pe", "act", "dve", "pool")}
        rings = {q: [es.enter_context(nc.semaphore(f"r_{q}{i}")) for i in range(RING)] for q in ("sp", "pool")}
        P = Prog(nc, esem, rings)
        C = dict(nc=nc, P=P, arena=Arena(abuf, ARENA), ps=ps, dram=dr)
        for ph in phases:
            PHASES[ph](C)
    return nc


PHASES = {"A": phase_A, "B": phase_B, "C": phase_C, "D": phase_D, "E": phase_E}


def kernel(**inputs):
    phases = os.environ.get("MK_PHASES", "ABCDE")
    debug = [s for s in os.environ.get("MK_DEBUG", "").split(",") if s]
    x = np.asarray(inputs["x"], np.float32)
    shared = {
        "w_in": np.ascontiguousarray(np.asarray(inputs["w_in"], np.float32)[0]),
        "w_out": np.ascontiguousarray(np.asarray(inputs["w_out"], np.float32)[0]),
        "w_up": np.ascontiguousarray(np.asarray(inputs["w_up"], np.float32)[0]),
        "w_down": np.ascontiguousarray(np.asarray(inputs["w_down"], np.float32)[0]),
    }
    lnp = np.stack([np.asarray(inputs[k], np.float32)[0] for k in ("ln1_g", "ln1_b", "ln2_g", "ln2_b")])
    shared["lnp"] = np.ascontiguousarray(lnp)
    cw = np.asarray(inputs["conv_w"], np.float32)[0]
    cb = np.asarray(inputs["conv_b"], np.float32)[0]
    cwb = np.concatenate([cw, cb[None, :]], axis=0)
    cwb = cwb.reshape(4, NFC, 128).transpose(2, 1, 0)
    shared["cwb"] = np.ascontiguousarray(cwb)
    shared["lamp"] = np.ascontiguousarray(np.stack(
        [np.asarray(inputs[k], np.float32)[0] for k in ("lambda_q1", "lambda_k1", "lambda_q2", "lambda_k2")]))
    shared["subg"] = np.ascontiguousarray(np.asarray(inputs["subln_g"], np.float32)[0])

    consts = [_core_consts(p) for p in range(2)]
    in_maps = []
    for core in range(8):
        b, p = core // 2, core % 2
        cc, own = consts[p]
        xb = x[b]
        m = dict(shared)
        m.update(cc)
        m["xT"] = np.ascontiguousarray(xb.T)
        xq = xb[own]
        m["xq"] = np.ascontiguousarray(xq)
        m["xTq"] = np.ascontiguousarray(xq.T)
        in_maps.append(m)
    nc = build_program(phases, debug)
    res = run_bass_kernel_spmd(nc, in_maps, core_ids=list(range(8)))
    if debug:
        kernel.debug = [{k: np.asarray(r[k]) for k in debug} for r in res.results]
    out = np.zeros((4, S, D), np.float32)
    for core in range(8):
        b, p = core // 2, core % 2
        o = np.asarray(res.results[core]["out"])
        for j in range(16):
            g = 2 * j + p
            out[b, g * 128:(g + 1) * 128] = o[j * 128:(j + 1) * 128]
    return out
```
